# Optimizing a Trainium2 kernel written in Bass

```python
import math
import jax
import jax.numpy as jnp
from jax import lax
import numpy as np

D_MODEL = 1024
BATCH = 32
SEQ = 2048
DEPTH = 2
DEC_BATCH = 32
DEC_SEQ = 32
PAST_LEN = 1024

CHUNK = 64
N_EVEN = (DEPTH + 1) // 2
N_ODD = DEPTH // 2
FOX_HEAD_DIM = 64
FOX_HEADS = D_MODEL // 128
FOX_WIDTH = FOX_HEADS * FOX_HEAD_DIM
FORGET_BIAS = 3.0
POOL_WINDOWS = (2, 4, 8, 16)
POOL_GROUPS = len(POOL_WINDOWS)
POOL_WIDTH = D_MODEL // 2
POOL_GROUP_DIM = POOL_WIDTH // POOL_GROUPS
POOL_HIST = max(POOL_WINDOWS) - 1
EVEN_IN = 3 * FOX_WIDTH + FOX_HEADS + POOL_WIDTH
MIX_WIDTH = FOX_WIDTH + POOL_WIDTH
SGU_CHUNK = 128
SGU_WIDTH = D_MODEL
SGU_GROUPS = 8
SGU_GROUP_DIM = SGU_WIDTH // SGU_GROUPS
D_FF = ((8 * D_MODEL // 3) + 255) // 256 * 256
Q_BLOCK = 128
EPS = 1e-6

kernel_name = "fox_pool_sgu_macaron_stream_step"


def rms_norm(x, g):
    xf = x.astype(jnp.float32)
    y = xf * lax.rsqrt(jnp.mean(xf * xf, axis=-1, keepdims=True) + EPS)
    return (y * g.astype(jnp.float32)).astype(x.dtype)


def swiglu(h, w_in, w_down):
    gate, up = jnp.split(h @ w_in, 2, axis=-1)
    return (jax.nn.silu(gate) * up) @ w_down


def even_proj(h, w_in, b_f):
    B, T, _ = h.shape
    z = h @ w_in
    q = z[..., :FOX_WIDTH].reshape(B, T, FOX_HEADS, FOX_HEAD_DIM)
    k = z[..., FOX_WIDTH:2 * FOX_WIDTH].reshape(B, T, FOX_HEADS, FOX_HEAD_DIM)
    v = z[..., 2 * FOX_WIDTH:3 * FOX_WIDTH].reshape(B, T, FOX_HEADS, FOX_HEAD_DIM)
    f_logit = z[..., 3 * FOX_WIDTH:3 * FOX_WIDTH + FOX_HEADS] + b_f
    logf = jax.nn.log_sigmoid(f_logit.astype(jnp.float32))
    u = z[..., 3 * FOX_WIDTH + FOX_HEADS:]
    return q, k, v, logf, u


def fox_attend(q, cq, qpos, k, ck, kpos, v):
    s = jnp.einsum('bqhd,bkhd->bhqk', q, k).astype(jnp.float32) * (FOX_HEAD_DIM ** -0.5)
    s = s + jnp.transpose(cq, (0, 2, 1))[:, :, :, None] - jnp.transpose(ck, (0, 2, 1))[:, :, None, :]
    s = jnp.where(kpos[None, :] <= qpos[:, None], s, -jnp.inf)
    p = jax.nn.softmax(s, axis=-1)
    return jnp.einsum('bhqk,bkhd->bqhd', p.astype(v.dtype), v)


def fox_prompt(q, k, v, logf):
    B, S, H, Dh = q.shape
    c = jnp.cumsum(logf.astype(jnp.float32), axis=1)
    nb = S // Q_BLOCK
    pos = jnp.arange(S)
    qb = q.reshape(B, nb, Q_BLOCK, H, Dh).transpose(1, 0, 2, 3, 4)
    cb = c.reshape(B, nb, Q_BLOCK, H).transpose(1, 0, 2, 3)
    pb = pos.reshape(nb, Q_BLOCK)
    out = lax.map(lambda a: fox_attend(a[0], a[1], a[2], k, c, pos, v), (qb, cb, pb))
    return out.transpose(1, 0, 2, 3, 4).reshape(B, S, H * Dh)


def fox_sample(q, k, v, logf, k_past, v_past, logf_past):
    B, T, H, Dh = q.shape
    P = k_past.shape[1]
    k_all = jnp.concatenate([k_past, k], axis=1)
    v_all = jnp.concatenate([v_past, v], axis=1)
    c = jnp.cumsum(jnp.concatenate([logf_past.astype(jnp.float32), logf], axis=1), axis=1)
    kpos = jnp.arange(P + T)
    qpos = P + jnp.arange(T)
    out = fox_attend(q, c[:, P:], qpos, k_all, c, kpos, v_all)
    return out.reshape(B, T, H * Dh)


def pool_mix(u_ext, pos, w_grp, scale):
    B = u_ext.shape[0]
    T = pos.shape[0]
    uf = u_ext.astype(jnp.float32)
    cs = jnp.cumsum(jnp.pad(uf, ((0, 0), (1, 0), (0, 0))), axis=1)
    end = cs[:, POOL_HIST + 1:]
    u_new = uf[:, POOL_HIST:]
    outs = []
    for g, w in enumerate(POOL_WINDOWS):
        sl = slice(g * POOL_GROUP_DIM, (g + 1) * POOL_GROUP_DIM)
        win_sum = end[:, :, sl] - cs[:, POOL_HIST + 1 - w:POOL_HIST + 1 - w + T, sl]
        cnt = jnp.minimum(w, pos + 1).astype(jnp.float32)[None, :, None]
        outs.append(win_sum / cnt - u_new[:, :, sl])
    d = jnp.stack(outs, axis=2)
    y = jnp.einsum('btgc,gcd->btgd', d, w_grp.astype(jnp.float32)).reshape(B, T, POOL_WIDTH)
    return (y * scale.astype(jnp.float32)).astype(u_ext.dtype)


def sgu_mask(n):
    idx = jnp.arange(n) // CHUNK
    return idx[None, :] <= idx[:, None]


def sgu_proj(h, w_in, norm_g):
    zu, zv = jnp.split(jax.nn.gelu(h @ w_in), 2, axis=-1)
    return zu, rms_norm(zv, norm_g)


def sgu_prompt(zu, zv, w_s, b_s):
    B, S, _ = zu.shape
    n = S // SGU_CHUNK
    w = jnp.where(sgu_mask(SGU_CHUNK)[None], w_s, 0)
    vb = zv.reshape(B, n, SGU_CHUNK, SGU_GROUPS, SGU_GROUP_DIM)
    mix = jnp.einsum('gts,bnsgc->bntgc', w, vb) + b_s.T[:, :, None]
    return zu * mix.reshape(B, S, SGU_WIDTH)


def sgu_sample(zu, zv, w_s, b_s):
    B, T, _ = zu.shape
    w = jnp.where(sgu_mask(T)[None], w_s[:, :T, :T], 0)
    vb = zv.reshape(B, T, SGU_GROUPS, SGU_GROUP_DIM)
    mix = jnp.einsum('gts,bsgc->btgc', w, vb) + b_s[:, :T].T[:, :, None]
    return zu * mix.reshape(B, T, SGU_WIDTH)


def setup_inputs(seed: int = 0) -> dict:
    key = jax.random.key(seed)
    ks = jax.random.split(key, 20)
    f32 = jnp.float32

    def nrm(k, shape, scale):
        return jax.random.normal(k, shape, f32) * scale

    return {
        "x_prompt": nrm(ks[0], (BATCH, SEQ, D_MODEL), 1.0),
        "x_sample": nrm(ks[1], (DEC_BATCH, DEC_SEQ, D_MODEL), 1.0),
        "cache_k": nrm(ks[2], (N_EVEN, DEC_BATCH, PAST_LEN, FOX_HEADS, FOX_HEAD_DIM), 1.0),
        "cache_v": nrm(ks[3], (N_EVEN, DEC_BATCH, PAST_LEN, FOX_HEADS, FOX_HEAD_DIM), 1.0),
        "cache_logf": jax.nn.log_sigmoid(FORGET_BIAS + nrm(ks[4], (N_EVEN, DEC_BATCH, PAST_LEN, FOX_HEADS), 0.5)),
        "state_pool": nrm(ks[5], (N_EVEN, DEC_BATCH, POOL_HIST, POOL_WIDTH), 1.0),
        "norm_g": 1.0 + nrm(ks[6], (DEPTH, 3, D_MODEL), 0.05),
        "ffn_w_in": nrm(ks[7], (DEPTH, 2, D_MODEL, 2 * D_FF), D_MODEL ** -0.5),
        "ffn_w_down": nrm(ks[8], (DEPTH, 2, D_FF, D_MODEL), D_FF ** -0.5),
        "even_w_in": nrm(ks[9], (N_EVEN, D_MODEL, EVEN_IN), D_MODEL ** -0.5),
        "even_b_f": FORGET_BIAS + nrm(ks[10], (N_EVEN, FOX_HEADS), 0.5),
        "pool_w": nrm(ks[11], (N_EVEN, POOL_GROUPS, POOL_GROUP_DIM, POOL_GROUP_DIM), POOL_GROUP_DIM ** -0.5),
        "pool_scale": 1.0 + nrm(ks[12], (N_EVEN, POOL_WIDTH), 0.1),
        "even_w_out": nrm(ks[13], (N_EVEN, MIX_WIDTH, D_MODEL), MIX_WIDTH ** -0.5),
        "sgu_w_in": nrm(ks[14], (N_ODD, D_MODEL, 2 * SGU_WIDTH), D_MODEL ** -0.5),
        "sgu_norm_g": 1.0 + nrm(ks[15], (N_ODD, SGU_WIDTH), 0.05),
        "sgu_w_s": nrm(ks[16], (N_ODD, SGU_GROUPS, SGU_CHUNK, SGU_CHUNK), SGU_CHUNK ** -0.5),
        "sgu_b_s": 1.0 + nrm(ks[17], (N_ODD, SGU_GROUPS, SGU_CHUNK), 0.1),
        "sgu_w_out": nrm(ks[18], (N_ODD, SGU_WIDTH, D_MODEL), SGU_WIDTH ** -0.5),
        "final_g": 1.0 + nrm(ks[19], (D_MODEL,), 0.05),
    }


def reference(x_prompt, x_sample, cache_k, cache_v, cache_logf, state_pool, norm_g, ffn_w_in, ffn_w_down,
              even_w_in, even_b_f, pool_w, pool_scale, even_w_out, sgu_w_in, sgu_norm_g, sgu_w_s, sgu_b_s,
              sgu_w_out, final_g):
    xp, xs = x_prompt, x_sample
    B, S, _ = xp.shape
    Bs, T, _ = xs.shape
    P = cache_k.shape[2]
    pos_p = jnp.arange(S)
    pos_s = P + jnp.arange(T)
    kp_l, vp_l, fp_l, up_l = [], [], [], []
    ks_l, vs_l, fs_l, us_l, zs_l = [], [], [], [], []
    for l in range(DEPTH):
        xp = xp + 0.5 * swiglu(rms_norm(xp, norm_g[l, 0]), ffn_w_in[l, 0], ffn_w_down[l, 0])
        xs = xs + 0.5 * swiglu(rms_norm(xs, norm_g[l, 0]), ffn_w_in[l, 0], ffn_w_down[l, 0])
        hp = rms_norm(xp, norm_g[l, 1])
        hs = rms_norm(xs, norm_g[l, 1])
        if l % 2 == 0:
            e = l // 2
            q, k, v, lf, u = even_proj(hp, even_w_in[e], even_b_f[e])
            att = fox_prompt(q, k, v, lf)
            u_ext = jnp.concatenate([jnp.zeros((B, POOL_HIST, POOL_WIDTH), u.dtype), u], axis=1)
            pool = pool_mix(u_ext, pos_p, pool_w[e], pool_scale[e])
            xp = xp + jnp.concatenate([att, pool], axis=-1) @ even_w_out[e]
            kp_l.append(k); vp_l.append(v); fp_l.append(lf); up_l.append(u_ext[:, -POOL_HIST:])
            q, k, v, lf, u = even_proj(hs, even_w_in[e], even_b_f[e])
            att = fox_sample(q, k, v, lf, cache_k[e], cache_v[e], cache_logf[e])
            u_ext = jnp.concatenate([state_pool[e], u], axis=1)
            pool = pool_mix(u_ext, pos_s, pool_w[e], pool_scale[e])
            xs = xs + jnp.concatenate([att, pool], axis=-1) @ even_w_out[e]
            ks_l.append(k); vs_l.append(v); fs_l.append(lf); us_l.append(u_ext[:, -POOL_HIST:])
        else:
            o = l // 2
            zu, zv = sgu_proj(hp, sgu_w_in[o], sgu_norm_g[o])
            xp = xp + sgu_prompt(zu, zv, sgu_w_s[o], sgu_b_s[o]) @ sgu_w_out[o]
            zu, zv = sgu_proj(hs, sgu_w_in[o], sgu_norm_g[o])
            xs = xs + sgu_sample(zu, zv, sgu_w_s[o], sgu_b_s[o]) @ sgu_w_out[o]
            zs_l.append(zv)
        xp = xp + 0.5 * swiglu(rms_norm(xp, norm_g[l, 2]), ffn_w_in[l, 1], ffn_w_down[l, 1])
        xs = xs + 0.5 * swiglu(rms_norm(xs, norm_g[l, 2]), ffn_w_in[l, 1], ffn_w_down[l, 1])
    y_prompt = rms_norm(xp, final_g)
    y_sample = rms_norm(xs, final_g)
    return (y_prompt, y_sample, jnp.stack(kp_l), jnp.stack(vp_l), jnp.stack(fp_l), jnp.stack(up_l),
            jnp.stack(ks_l), jnp.stack(vs_l), jnp.stack(fs_l), jnp.stack(us_l), jnp.stack(zs_l))
```

```python
import numpy as np
from contextlib import ExitStack
import concourse.bass as bass
import concourse.mybir as mybir
from concourse.bass_utils import run_bass_kernel_spmd

F32 = mybir.dt.float32
BF16 = mybir.dt.bfloat16
AF = mybir.ActivationFunctionType
ALU = mybir.AluOpType

D = 1024
DFF = 2816
NGRP = 11
H = 8
DH = 64
EVEN_IN = 2056
EPS = 1e-6
SAME_ENGINE_SYNC = True


class Trk:
    def __init__(self, nc, es):
        self.nc = nc
        self.es = es
        self.E = {'pe': nc.tensor, 'act': nc.scalar, 'dve': nc.vector, 'pool': nc.gpsimd, 'sp': nc.sync}
        self.sem = {}
        self.cnt = {}
        self.waited = {e: {} for e in self.E}
        for e in self.E:
            self.sem[e] = es.enter_context(nc.semaphore("s_" + e))
            self.cnt[e] = 0
        self.lastw = {}
        self.readers = {}
        self.dsem = {}
        self.nwait = 0
        self.dead = False

    def _wait(self, e, r, w):
        need = {}
        for k in list(r) + list(w):
            if k in self.lastw:
                s, v = self.lastw[k]
                if s == e and (e == 'pe' or not SAME_ENGINE_SYNC):
                    continue
                need[s] = max(need.get(s, 0), v)
        for k in w:
            for (s, v) in self.readers.get(k, ()):
                if s == e:
                    continue
                need[s] = max(need.get(s, 0), v)
        for s, v in need.items():
            if self.waited[e].get(s, 0) >= v:
                continue
            if s in self.E:
                h = self.sem[s]
            else:
                d = self.dsem[s]
                assert d[1] == v, "dma sem %s waited at %d but issued %d" % (s, v, d[1])
                h = d[0]
            self.E[e].wait_ge(h, v)
            self.waited[e][s] = v
            self.nwait += 1

    def _mark(self, tok, r, w):
        for k in r:
            self.readers.setdefault(k, []).append(tok)
        for k in w:
            self.lastw[k] = tok
            self.readers[k] = []

    def op(self, e, fn, r=(), w=()):
        if self.dead:
            return
        self._wait(e, r, w)
        inst = fn()
        inst.then_inc(self.sem[e], 1)
        self.cnt[e] += 1
        self._mark((e, self.cnt[e]), r, w)

    def mms(self, fns, r=(), w=()):
        e = 'pe'
        if self.dead:
            return
        self._wait(e, r, w)
        inst = None
        for fn in fns:
            inst = fn()
        inst.then_inc(self.sem[e], 1)
        self.cnt[e] += 1
        self._mark((e, self.cnt[e]), r, w)

    def dma(self, q, out, in_, r=(), w=(), sem=None):
        if self.dead:
            return
        self._wait(q, r, w)
        if sem not in self.dsem:
            self.dsem[sem] = [self.es.enter_context(self.nc.semaphore("d_" + sem)), 0]
        d = self.dsem[sem]
        d[1] += 16
        self.E[q].dma_start(out=out, in_=in_).then_inc(d[0], 16)
        self._mark((sem, d[1]), r, w)

    def barrier(self):
        if self.dead:
            return
        for e in self.E:
            for s in self.E:
                if s != e and self.cnt[s] > self.waited[e].get(s, 0):
                    self.E[e].wait_ge(self.sem[s], self.cnt[s])
                    self.waited[e][s] = self.cnt[s]
            for s, d in self.dsem.items():
                if d[1] > self.waited[e].get(s, 0):
                    self.E[e].wait_ge(d[0], d[1])
                    self.waited[e][s] = d[1]
        self.lastw = {}
        self.readers = {}

    def finish(self):
        for s, d in self.dsem.items():
            if d[1] > self.waited['sp'].get(s, 0):
                self.nc.sync.wait_ge(d[0], d[1])
        for s in self.E:
            if s != 'sp' and self.cnt[s] > self.waited['sp'].get(s, 0):
                self.nc.sync.wait_ge(self.sem[s], self.cnt[s])


def bc(ap, shape):
    return ap.to_broadcast(list(shape))


class StopBuild(Exception):
    pass


class Prog:
    def chk(self, name):
        if self.stop_at == name and not self.t.dead:
            self.t.barrier()
            self.t.dead = True

    def __init__(self, NP, S, PL, NS, T, P, stop_at=None):
        self.stop_at = stop_at
        self.NP, self.S, self.PL, self.NS, self.T, self.P = NP, S, PL, NS, T, P
        assert NS * T == 128 and S % PL == 0 and PL % 128 == 0 and P % 128 == 0
        self.NBP = PL // 128
        self.NH = S // PL
        self.NBS = S // 128
        self.NBC = P // 128
        self.TW = min(512, PL)
        self.SB = self.NBP
        self.HC = PL + 128
        self.nc = bass.Bass("TRN2", target_bir_lowering=False)
        self.yq = 0
        self.wq = 0
        self.tq = 0
        self.build()

    def dram(self):
        nc = self.nc
        NP, S, NS, T, P = self.NP, self.S, self.NS, self.T, self.P

        def I(name, shape):
            return nc.dram_tensor(name, list(shape), F32, kind="ExternalInput").ap()

        def O(name, shape):
            return nc.dram_tensor(name, list(shape), F32, kind="ExternalOutput").ap()

        self.xp = I("xp", [NP, S, D])
        self.xs = I("xs", [NS * T, D])
        self.ck = I("ck", [NS, P, 512])
        self.cv = I("cv", [NS, P, 512])
        self.clf = I("clf", [NS, P, H])
        self.spool = I("spool", [NS, 15, 512])
        self.norm_g = I("norm_g", [2, 3, D])
        self.ffn_w_in = I("ffn_w_in", [2, 2, D, 2 * DFF])
        self.ffn_w_down = I("ffn_w_down", [2, 2, DFF, D])
        self.even_w_in = I("even_w_in", [D, EVEN_IN])
        self.even_b_f = I("even_b_f", [H])
        self.pool_w = I("pool_w", [4, 128, 128])
        self.pool_scale = I("pool_scale", [512])
        self.even_w_out = I("even_w_out", [D, D])
        self.sgu_w_in = I("sgu_w_in", [D, 2 * D])
        self.sgu_norm_g = I("sgu_norm_g", [D])
        self.sgu_w_s = I("sgu_w_s", [8, 128, 128])
        self.sgu_b_s = I("sgu_b_s", [8, 128])
        self.sgu_w_out = I("sgu_w_out", [D, D])
        self.final_g = I("final_g", [D])
        self.y_p = O("y_p", [NP, S, D])
        self.y_s = O("y_s", [NS * T, D])
        self.nk_p = O("nk_p", [NP, S, 512])
        self.nv_p = O("nv_p", [NP, S, 512])
        self.nlf_p = O("nlf_p", [NP, S, H])
        self.npool_p = O("npool_p", [NP, 15, 512])
        self.nk_s = O("nk_s", [NS, T, 512])
        self.nv_s = O("nv_s", [NS, T, 512])
        self.nlf_s = O("nlf_s", [NS, T, H])
        self.npool_s = O("npool_s", [NS, 15, 512])
        self.nz_s = O("nz_s", [NS * T, D])

    def sb(self, es, name, shape, dt=F32):
        self.uid = getattr(self, 'uid', 0) + 1
        return es.enter_context(self.nc.sbuf_tensor("%s_%d" % (name, self.uid), list(shape), dt))

    def psb(self, i):
        return self.ps[:, i * 512:(i + 1) * 512]

    def psb16(self, i):
        return self.ps[:, i * 512:(i + 1) * 512].bitcast(BF16)

    def build(self):
        nc = self.nc
        self.dram()
        with ExitStack() as es:
            self.t = t = Trk(nc, es)
            sb = lambda n, sh, dt=F32: self.sb(es, n, sh, dt)
            self.ps = es.enter_context(nc.psum_tensor("ps", [128, 4096], F32))
            self.x = sb("x", [128, self.NBP + 1, D])
            self.hT = sb("hT", [128, 8, self.HC], BF16)
            self.KT = sb("KT", [128, 4, self.S], BF16)
            self.Vaug = sb("Vaug", [128, self.NBS, H, 65], BF16)
            self.ctm = sb("ctm", [128, self.NBS, H])
            self.carry = sb("carry", [128, H])
            self.hist = sb("hist", [128, 4, 16])
            self.gb = sb("gb", [128, D])
            self.ss = sb("ss", [128, 16])
            self.rt = sb("rt", [128, 16])
            self.rstd = sb("rstd", [128, 16])
            self.junk = sb("junk", [128, D], BF16)
            self.gbuf = [sb("gbuf%d" % k, [128, D]) for k in range(2)]
            self.hbn = [sb("hbn%d" % k, [128, D], BF16) for k in range(2)]
            self.nst = sb("nst", [128, 8])
            self.nst2 = sb("nst2", [128, 3, 8])
            self.gq = 0
            self.nq = 0
            self.norm2banks = False
            self.fq = 0
            self.nq2 = 0
            self.hq = 0
            self.consts(es)
            npass = 0
            try:
                self.chk("consts")
                for i in range(self.NP):
                    for hf in range(self.NH):
                        self.do_pass(i, hf, ride=(npass == 0))
                        npass += 1
            except StopBuild:
                pass
            t.dead = False
            t.barrier()
            t.finish()

    def consts(self, es):
        nc, t = self.nc, self.t
        sb = lambda n, sh, dt=F32: self.sb(es, n, sh, dt)
        self.allones = sb("allones", [128, 128])
        self.identf = sb("identf", [128, 128])
        self.identb = sb("identb", [128, 128], BF16)
        self.triinc = sb("triinc", [128, 128])
        self.trimask = sb("trimask", [128, 128], BF16)
        self.sel64 = sb("sel64", [128, 128])
        self.sel16 = sb("sel16", [128, 128])
        self.sel0 = sb("sel0", [128, 128])
        self.eps_t = sb("eps_t", [128, 1])
        self.one_t = sb("one_t", [128, 1])
        self.invc = sb("invc", [128, 16])
        self.mhalf = sb("mhalf", [128, 1])
        P = lambda fn, r=(), w=(): t.op('pool', fn, r, w)
        P(lambda: nc.gpsimd.memset(self.allones[:], 1.0), w=['allones'])
        P(lambda: nc.gpsimd.memset(self.eps_t[:], EPS), w=['c_eps'])
        P(lambda: nc.gpsimd.memset(self.one_t[:], 1.0), w=['c_one'])
        P(lambda: nc.gpsimd.memset(self.mhalf[:], -0.5), w=['c_mhalf'])
        for j in range(16):
            P(lambda j=j: nc.gpsimd.memset(self.invc[:, j:j + 1], 1.0 / (j + 1)), w=['c_invc'])
        P(lambda: nc.gpsimd.affine_select(out=self.identf[:], in_=self.allones[:], pattern=[[1, 128]],
                                          compare_op=ALU.is_equal, fill=0.0, base=0, channel_multiplier=-1),
          r=['allones'], w=['identf'])
        P(lambda: nc.gpsimd.affine_select(out=self.triinc[:], in_=self.allones[:], pattern=[[1, 128]],
                                          compare_op=ALU.is_ge, fill=0.0, base=0, channel_multiplier=-1),
          r=['allones'], w=['triinc'])
        P(lambda: nc.gpsimd.affine_select(out=self.sel64[:], in_=self.allones[:], pattern=[[0, 128]],
                                          compare_op=ALU.is_equal, fill=0.0, base=-64, channel_multiplier=1),
          r=['allones'], w=['sel64'])
        P(lambda: nc.gpsimd.affine_select(out=self.sel16[:], in_=self.allones[:], pattern=[[0, 128]],
                                          compare_op=ALU.is_equal, fill=0.0, base=-16, channel_multiplier=1),
          r=['allones'], w=['sel16'])
        P(lambda: nc.gpsimd.affine_select(out=self.sel0[:], in_=self.allones[:], pattern=[[0, 128]],
                                          compare_op=ALU.is_equal, fill=0.0, base=0, channel_multiplier=1),
          r=['allones'], w=['sel0'])
        P(lambda: nc.gpsimd.tensor_copy(out=self.identb[:], in_=self.identf[:]), r=['identf'], w=['identb'])
        P(lambda: nc.gpsimd.tensor_copy(out=self.trimask[:], in_=self.triinc[:]), r=['triinc'], w=['trimask'])
        P(lambda: nc.gpsimd.memset(self.nst[:], 0.0), w=['nst0', 'nst1'])
        P(lambda: nc.gpsimd.memset(self.nst2[:], 0.0), w=['nz0', 'nz1'])
        P(lambda: nc.gpsimd.memset(self.Vaug[:, :, :, 64:65], 1.0), w=['vones'])
        t.barrier()

    def xk(self, b):
        return ['x%d_0' % b, 'x%d_1' % b]

    def do_pass(self, i, hf, ride):
        nc, t = self.nc, self.t
        NBP, PL = self.NBP, self.PL
        blocks = list(range(NBP)) + ([self.SB] if ride else [])
        pos0 = hf * PL
        pre = getattr(self, 'x_pre', set())
        for b in range(NBP):
            if b in pre:
                continue
            t.dma('sp', out=self.x[:, b, :], in_=self.xp[i, pos0 + b * 128:pos0 + (b + 1) * 128, :], w=self.xk(b), sem='xin_%d' % b)
        self.x_pre = set()
        nxt_pass = (i, hf + 1) if hf + 1 < self.NH else ((i + 1, 0) if i + 1 < self.NP else None)
        self.next_x = nxt_pass
        if ride:
            t.dma('sp', out=self.x[:, self.SB, :], in_=self.xs, w=self.xk(self.SB), sem='xin2')
        tiles = []
        for c0 in range(0, PL, self.TW):
            tiles.append((c0, self.TW, list(range(c0 // 128, (c0 + self.TW) // 128))))
        if ride:
            tiles.append((PL, 128, [self.SB]))
        self.chk("xload")
        if getattr(self, 'pre_normed', False):
            self.pre_normed = False
        else:
            self.norm_to_hT(self.norm_g[0, 0], blocks)
        last_pass = (i == self.NP - 1 and hf == self.NH - 1)
        for l in range(2):
            gk = self.load_gain(self.norm_g[l, 1])
            self.ffn_open()
            self.ffn(l, 0, blocks, tiles, post=lambda bl, gk=gk: self.norm_blocks(bl, gk), nxt=None)
            self.ffn_close()
            self.chk("ffn%d0" % l)
            gk = self.load_gain(self.norm_g[l, 2])
            post = lambda bl, gk=gk: self.norm_blocks(bl, gk)
            if l == 0:
                self.even_prompt(i, hf, post)
                self.chk("evenp")
                if ride:
                    self.even_sample(post)
                    self.chk("evens")
            else:
                self.odd(i, hf, blocks, post, gk)
                self.chk("odd")
            self.ffn_open()
            if l == 0:
                gk = self.load_gain(self.norm_g[1, 0])
                self.ffn(l, 1, blocks, tiles, post=lambda bl, gk=gk: self.norm_blocks(bl, gk), nxt=(1, 0))
            else:
                gk = self.load_gain(self.final_g)
                gk_first = None if last_pass else self.load_gain(self.norm_g[0, 0])
                self.ffn(l, 1, blocks, tiles, final=(gk, i, hf, gk_first), nxt=(None if last_pass else (0, 0)))
                if gk_first is not None:
                    self.pre_normed = True
        if last_pass:
            self.ffn_close()
        self.chk("pass")

    def ffn_open(self):
        if getattr(self, 'fes', None) is not None:
            return
        self.fes = es = ExitStack()
        es.__enter__()
        fb = {}
        fb['stin'] = [self.sb(es, "stin%d" % k, [128, 8, 2, 256]) for k in range(2)]
        fb['stdn'] = [self.sb(es, "stdn%d" % k, [128, 2, D]) for k in range(2)]
        fb['wbi'] = [self.sb(es, "wbi%d" % k, [128, 8, 2, 256], BF16) for k in range(2)]
        fb['wbd'] = [self.sb(es, "wbd%d" % k, [128, 2, D], BF16) for k in range(2)]
        fb['sg'] = [self.sb(es, "sg%d" % k, [128, 512]) for k in range(2)]
        fb['act'] = [[self.sb(es, "act%d_%d" % (p, m), [128, 512], BF16) for m in range(2)] for p in range(2)]
        fb['yst'] = [self.sb(es, "yst%d" % k, [128, D]) for k in range(2)]
        self.fb = fb
        self.preloaded = 0
        self.precast = 0

    def ffn_close(self):
        self.t.barrier()
        self.fes.__exit__(None, None, None)
        self.fes = None

    def rstd_of(self, srcs, keys_list, n_feat=D):
        nc, t = self.nc, self.t
        n = len(srcs)
        t.op('dve', lambda: nc.vector.memset(self.ss[:, 0:n], 0.0), w=['ss'])
        for j, (a, ks) in enumerate(zip(srcs, keys_list)):
            t.op('act', lambda a=a, j=j: nc.scalar.activation(out=self.junk[:, 0:n_feat], in_=a, func=AF.Square,
                                                             accum_out=self.ss[:, j:j + 1]),
                 r=list(ks) + ['ss'], w=['ssw%d' % j])
        t.op('act', lambda: nc.scalar.activation(out=self.rt[:, 0:n], in_=self.ss[:, 0:n], func=AF.Sqrt,
                                                 bias=self.eps_t[:], scale=1.0 / n_feat),
             r=['ssw%d' % j for j in range(n)] + ['c_eps', 'ss'], w=['rt'])
        t.op('dve', lambda: nc.vector.reciprocal(out=self.rstd[:, 0:n], in_=self.rt[:, 0:n]), r=['rt'], w=['rstd'])

    def load_gain(self, gain_ap):
        k = self.gq % 2
        self.gq += 1
        self.t.dma('sp', out=self.gbuf[k][:], in_=gain_ap.partition_broadcast(128), w=['gbuf%d' % k], sem='gbf%d' % k)
        return k

    def rstd_block(self, b):
        nc, t = self.nc, self.t
        k = self.nq % 2
        self.nq += 1
        nst = self.nst
        t.op('act', lambda: nc.scalar.activation(out=self.junk[:], in_=self.x[:, b, :], func=AF.Square, accum_out=nst[:, k:k + 1]),
             r=self.xk(b) + ['nst%d' % k], w=['nsq%d' % k])
        t.op('dve', lambda: nc.vector.tensor_scalar(out=nst[:, 2 + k:3 + k], in0=nst[:, k:k + 1], scalar1=1.0 / D, scalar2=EPS,
                                                    op0=ALU.mult, op1=ALU.add), r=['nsq%d' % k], w=['nrt%d' % k])
        t.op('pool', lambda: nc.gpsimd.tensor_tensor(out=nst[:, 4 + k:5 + k], in0=nst[:, 2 + k:3 + k], in1=self.mhalf[:], op=ALU.pow),
             r=['nrt%d' % k, 'c_mhalf'], w=['nrs%d' % k])
        t.op('dve', lambda: nc.vector.memset(nst[:, k:k + 1], 0.0), r=['nsq%d' % k, 'nrt%d' % k], w=['nst%d' % k])
        return k

    def _rstd_list(self, bl):
        nc, t = self.nc, self.t
        n = len(bl)
        assert n <= 4
        par = self.nq2 % 2
        self.nq2 += 1
        base = par * 4
        st = self.nst2
        for j, b in enumerate(bl):
            t.op('act', lambda j=j, b=b: nc.scalar.activation(out=self.junk[:], in_=self.x[:, b, :], func=AF.Square,
                                                             accum_out=st[:, 0, base + j:base + j + 1]),
                 r=self.xk(b) + ['nz%d' % par], w=['nsq%d_%d' % (par, j)])
        sq = ['nsq%d_%d' % (par, j) for j in range(n)]
        t.op('dve', lambda: nc.vector.tensor_scalar(out=st[:, 1, base:base + n], in0=st[:, 0, base:base + n], scalar1=1.0 / D,
                                                    scalar2=EPS, op0=ALU.mult, op1=ALU.add), r=sq, w=['nrt2_%d' % par])
        t.op('pool', lambda: nc.gpsimd.tensor_tensor(out=st[:, 2, base:base + n], in0=st[:, 1, base:base + n],
                                                     in1=bc(self.mhalf[:], [128, n]), op=ALU.pow),
             r=['nrt2_%d' % par, 'c_mhalf'], w=['nrs2_%d' % par])
        t.op('dve', lambda: nc.vector.memset(st[:, 0, base:base + n], 0.0), r=sq + ['nrt2_%d' % par], w=['nz%d' % par])
        return st, base, par

    def norm_blocks(self, bl, gk):
        nc, t = self.nc, self.t
        st, base, par = self._rstd_list(bl)
        for j, b in enumerate(bl):
            k = self.hq % 2
            self.hq += 1
            t.op('dve', lambda j=j, b=b, k=k: nc.vector.scalar_tensor_tensor(
                out=self.hbn[k][:], in0=self.x[:, b, :], scalar=st[:, 2, base + j:base + j + 1], in1=self.gbuf[gk][:],
                op0=ALU.mult, op1=ALU.mult), r=self.xk(b) + ['nrs2_%d' % par, 'gbuf%d' % gk], w=['hbn%d' % k])
            nbk = 7 - (k if self.norm2banks else 0)
            pst = self.psb16(nbk)
            t.mms([lambda c=c, k=k, pst=pst: nc.tensor.transpose(out=pst[:, c * 128:(c + 1) * 128], in_=self.hbn[k][:, c * 128:(c + 1) * 128],
                                                                 identity=self.identb[:]) for c in range(8)], r=['hbn%d' % k, 'identb'], w=['ps%d' % nbk])
            t.op('act', lambda b=b, pst=pst: nc.scalar.copy(out=self.hT[:, :, b * 128:(b + 1) * 128], in_=pst.rearrange("p (c t) -> p c t", c=8)),
                 r=['ps%d' % nbk], w=['hT%d' % b])

    def norm_pre(self, b, gk):
        nc, t = self.nc, self.t
        par = self.nq2 % 2
        self.nq2 += 1
        base = par * 4
        st = self.nst2
        t.op('act', lambda: nc.scalar.activation(out=self.junk[:], in_=self.x[:, b, :], func=AF.Square,
                                                 accum_out=st[:, 0, base:base + 1]),
             r=self.xk(b) + ['nz%d' % par], w=['nsq%d_0' % par])
        t.op('dve', lambda: nc.vector.tensor_scalar(out=st[:, 1, base:base + 1], in0=st[:, 0, base:base + 1], scalar1=1.0 / D,
                                                    scalar2=EPS, op0=ALU.mult, op1=ALU.add), r=['nsq%d_0' % par], w=['nrt2_%d' % par])
        t.op('pool', lambda: nc.gpsimd.tensor_tensor(out=st[:, 2, base:base + 1], in0=st[:, 1, base:base + 1],
                                                     in1=self.mhalf[:], op=ALU.pow), r=['nrt2_%d' % par, 'c_mhalf'], w=['nrs2_%d' % par])
        t.op('dve', lambda: nc.vector.memset(st[:, 0, base:base + 1], 0.0), r=['nsq%d_0' % par, 'nrt2_%d' % par], w=['nz%d' % par])
        k = self.hq % 2
        self.hq += 1
        t.op('dve', lambda: nc.vector.scalar_tensor_tensor(
            out=self.hbn[k][:], in0=self.x[:, b, :], scalar=st[:, 2, base:base + 1], in1=self.gbuf[gk][:],
            op0=ALU.mult, op1=ALU.mult), r=self.xk(b) + ['nrs2_%d' % par, 'gbuf%d' % gk], w=['hbn%d' % k])
        return (b, k)

    def norm_post(self, ctx):
        nc, t = self.nc, self.t
        b, k = ctx
        pst = self.psb16(7)
        t.mms([lambda c=c: nc.tensor.transpose(out=pst[:, c * 128:(c + 1) * 128], in_=self.hbn[k][:, c * 128:(c + 1) * 128],
                                               identity=self.identb[:]) for c in range(8)], r=['hbn%d' % k, 'identb'], w=['ps7'])
        t.op('act', lambda: nc.scalar.copy(out=self.hT[:, :, b * 128:(b + 1) * 128], in_=pst.rearrange("p (c t) -> p c t", c=8)),
             r=['ps7'], w=['hT%d' % b])

    def final_blocks(self, bl, gk, yst, i, hf):
        nc, t = self.nc, self.t
        st, base, par = self._rstd_list(bl)
        for j, b in enumerate(bl):
            k = self.fq % 2
            self.fq += 1
            t.op('dve', lambda j=j, b=b, k=k: nc.vector.scalar_tensor_tensor(
                out=yst[k][:], in0=self.x[:, b, :], scalar=st[:, 2, base + j:base + j + 1], in1=self.gbuf[gk][:],
                op0=ALU.mult, op1=ALU.mult), r=self.xk(b) + ['nrs2_%d' % par, 'gbuf%d' % gk], w=['yst%d' % k])
            if b == self.SB:
                dst = self.y_s
            else:
                p0 = hf * self.PL + b * 128
                dst = self.y_p[i, p0:p0 + 128, :]
            t.dma('sp', out=dst, in_=yst[k][:], r=['yst%d' % k], sem='oy%d' % k)
            nx = getattr(self, 'next_x', None)
            if nx is not None and b != self.SB:
                i2, hf2 = nx
                p2 = hf2 * self.PL + b * 128
                t.dma('sp', out=self.x[:, b, :], in_=self.xp[i2, p2:p2 + 128, :], w=self.xk(b), sem='xin_%d' % b)
                self.x_pre.add(b)

    def final_block(self, b, gk, yst, i, hf):
        nc, t = self.nc, self.t
        k = self.rstd_block(b)
        nst = self.nst
        t.op('dve', lambda: nc.vector.scalar_tensor_tensor(out=yst[k][:], in0=self.x[:, b, :], scalar=nst[:, 4 + k:5 + k],
                                                           in1=self.gbuf[gk][:], op0=ALU.mult, op1=ALU.mult),
             r=self.xk(b) + ['nrs%d' % k, 'gbuf%d' % gk], w=['yst%d' % k])
        if b == self.SB:
            dst = self.y_s
        else:
            p0 = hf * self.PL + b * 128
            dst = self.y_p[i, p0:p0 + 128, :]
        t.dma('sp', out=dst, in_=yst[k][:], r=['yst%d' % k], sem='oy%d' % k)
        nx = getattr(self, 'next_x', None)
        if nx is not None and b != self.SB:
            i2, hf2 = nx
            p2 = hf2 * self.PL + b * 128
            t.dma('sp', out=self.x[:, b, :], in_=self.xp[i2, p2:p2 + 128, :], w=self.xk(b), sem='xin_%d' % b)
            self.x_pre.add(b)

    def norm_to_hT(self, gain_ap, blocks):
        nc, t = self.nc, self.t
        with ExitStack() as es:
            hb = [self.sb(es, "hb%d" % k, [128, D], BF16) for k in range(2)]
            t.dma('sp', out=self.gb[:], in_=gain_ap.partition_broadcast(128), w=['gb'], sem='gb')
            self.rstd_of([self.x[:, b, :] for b in blocks], [self.xk(b) for b in blocks])
            for j, b in enumerate(blocks):
                k = j % 2
                t.op('dve', lambda b=b, j=j, k=k: nc.vector.scalar_tensor_tensor(
                    out=hb[k][:], in0=self.x[:, b, :], scalar=self.rstd[:, j:j + 1], in1=self.gb[:],
                    op0=ALU.mult, op1=ALU.mult), r=self.xk(b) + ['rstd', 'gb'], w=['hb%d' % k])
                bank = 6 + k
                pst = self.psb16(bank)
                t.mms([lambda c=c, k=k, pst=pst: nc.tensor.transpose(out=pst[:, c * 128:(c + 1) * 128],
                                                                  in_=hb[k][:, c * 128:(c + 1) * 128],
                                                                  identity=self.identb[:]) for c in range(8)],
                      r=['hb%d' % k, 'identb'], w=['ps%d' % bank])
                t.op('act', lambda b=b, pst=pst: nc.scalar.copy(out=self.hT[:, :, b * 128:(b + 1) * 128],
                                                              in_=pst.rearrange("p (c t) -> p c t", c=8)),
                     r=['ps%d' % bank], w=['hT%d' % b])
            t.barrier()

    def ffn(self, l, f, blocks, tiles, post=None, final=None, nxt=None):
        nc, t = self.nc, self.t
        fb = self.fb
        stin, stdn, wbi, wbd, sg, act = fb['stin'], fb['stdn'], fb['wbi'], fb['wbd'], fb['sg'], fb['act']
        self.norm2banks = True
        if final is not None:
            yst = fb['yst']

            pend_norm = []

            def post(bl):
                self.final_blocks(bl, final[0], yst, final[1], final[2])
                pb = [b for b in bl if b != self.SB]
                if final[3] is not None and pb:
                    pend_norm.append(pb)
        g0 = self.wq
        self.wq += NGRP

        def load(lf, g, gg):
            k = gg % 2
            win = self.ffn_w_in[lf[0], lf[1]].rearrange("(kc p) (two ff) -> p kc two ff", p=128, two=2)
            wdn = self.ffn_w_down[lf[0], lf[1]]
            for gu in range(2):
                t.dma('sp', out=stin[k][:, :, gu, :], in_=win[:, :, gu, g * 256:(g + 1) * 256], w=['stin%d' % k], sem='sti%d' % k)
            t.dma('sp', out=stdn[k][:], in_=wdn[g * 256:(g + 1) * 256, :].rearrange("(m p) d -> p m d", p=128),
                  w=['stdn%d' % k], sem='std%d' % k)

        def cast(gg):
            k = gg % 2
            t.op('act', lambda: nc.scalar.copy(out=wbi[k][:], in_=stin[k][:]), r=['stin%d' % k], w=['wbi%d' % k])
            t.op('act', lambda: nc.scalar.copy(out=wbd[k][:], in_=stdn[k][:]), r=['stdn%d' % k], w=['wbd%d' % k])

        def inproj(gg, ti, par):
            k = gg % 2
            c0, w, tb = tiles[ti]
            for m in range(2):
                for gu in range(2):
                    bank = m * 2 + gu
                    t.mms([lambda kc=kc, gu=gu, m=m, bank=bank: nc.tensor.matmul(
                        self.psb(bank)[:, 0:w], lhsT=wbi[k][:, kc, gu, m * 128:(m + 1) * 128],
                        rhs=self.hT[:, kc, c0:c0 + w], start=(kc == 0), stop=(kc == 7)) for kc in range(8)],
                        r=['wbi%d' % k] + ['hT%d' % b for b in tb], w=['ps%d' % bank])
                t.op('act', lambda m=m: nc.scalar.activation(out=sg[m][:, 0:w], in_=self.psb(m * 2)[:, 0:w], func=AF.Silu),
                     r=['ps%d' % (m * 2)], w=['sg%d' % m])
                t.op('dve', lambda m=m: nc.vector.tensor_tensor(out=act[par][m][:, 0:w], in0=sg[m][:, 0:w],
                                                               in1=self.psb(m * 2 + 1)[:, 0:w], op=ALU.mult),
                     r=['sg%d' % m, 'ps%d' % (m * 2 + 1)], w=['act%d_%d' % (par, m)])

        pend = []

        def down(g, gg, ti, par):
            k = gg % 2
            c0, w, tb = tiles[ti]
            for bi, b in enumerate(tb):
                for dh in range(2):
                    yb = 4 + (self.yq % 4)
                    self.yq += 1
                    t.mms([lambda m=m, bi=bi, dh=dh, yb=yb: nc.tensor.matmul(
                        self.psb(yb), lhsT=act[par][m][:, bi * 128:(bi + 1) * 128],
                        rhs=wbd[k][:, m, dh * 512:(dh + 1) * 512], start=(m == 0), stop=(m == 1)) for m in range(2)],
                        r=['act%d_%d' % (par, 0), 'act%d_%d' % (par, 1), 'wbd%d' % k], w=['ps%d' % yb])
                    xs = self.x[:, b, dh * 512:(dh + 1) * 512]
                    t.op('dve', lambda xs=xs, yb=yb: nc.vector.scalar_tensor_tensor(
                        out=xs, in0=self.psb(yb), scalar=0.5, in1=xs, op0=ALU.mult, op1=ALU.add),
                        r=['ps%d' % yb, 'x%d_%d' % (b, dh)], w=['x%d_%d' % (b, dh)])
            if g == NGRP - 1 and post is not None:
                if pend:
                    post(list(pend))
                pend[:] = list(tb)

        for g in range(self.preloaded, 2):
            load((l, f), g, g0 + g)
        precast = self.precast
        self.preloaded = 0
        self.precast = 0
        units = [(g, ti) for g in range(NGRP) for ti in range(len(tiles))]
        prev = None
        for u, (g, ti) in enumerate(units):
            if ti == 0:
                if g >= precast:
                    cast(g0 + g)
                if g + 2 < NGRP:
                    load((l, f), g + 2, g0 + g + 2)
                elif nxt is not None:
                    load(nxt, g + 2 - NGRP, g0 + g + 2)
                    self.preloaded += 1
            inproj(g0 + g, ti, u % 2)
            if prev is not None:
                down(prev[0], g0 + prev[0], prev[1], (u - 1) % 2)
                if nxt is not None and prev == (NGRP - 2, len(tiles) - 1):
                    cast(g0 + NGRP)
                    self.precast = 1
            prev = (g, ti)
        down(prev[0], g0 + prev[0], prev[1], (len(units) - 1) % 2)
        if post is not None and pend:
            post(list(pend))
        if final is not None:
            for pb in pend_norm:
                self.norm_blocks(pb, final[3])
        if nxt is not None:
            cast(g0 + NGRP + 1)
            self.precast = 2
        self.norm2banks = False

    def load_w_bf16(self, es, name, src_ap, ncols, stage, wt=None):
        nc, t = self.nc, self.t
        if wt is None:
            wt = self.sb(es, name, [128, 8, ncols], BF16)
        if not hasattr(self, 'scr'):
            self.scr = {}
            self.castq = 0
        if name not in self.scr:
            scr = nc.dram_tensor("scr_" + name, [128, 8 * ncols], BF16, kind="Internal").ap()
            self.scr[name] = scr
            t.dma('sp', out=stage[:, :, 0:ncols], in_=src_ap.rearrange("(kc p) c -> p kc c", p=128), w=['wstage'], sem='wst')
            self.castq += 1
            if self.castq % 2 == 0:
                t.op('act', lambda: nc.scalar.copy(out=wt[:], in_=stage[:, :, 0:ncols]), r=['wstage'], w=[name])
            else:
                t.op('dve', lambda: nc.vector.tensor_copy(out=wt[:], in_=stage[:, :, 0:ncols]), r=['wstage'], w=[name])
            t.dma('sp', out=scr.rearrange("p (kc c) -> p kc c", kc=8), in_=wt[:], r=[name], w=['scr_' + name], sem='scw_' + name)
        else:
            t.dma('sp', out=wt[:], in_=self.scr[name].rearrange("p (kc c) -> p kc c", kc=8), r=['scr_' + name], w=[name],
                  sem='wl_' + name)
        return wt

    def cumsum_blocks(self, lf, nb, cdst, carry_in, update_carry=True):
        nc, t = self.nc, self.t
        n = nb * H
        t.mms([lambda: nc.tensor.matmul(self.psb(0)[:, 0:n], lhsT=self.triinc[:], rhs=lf.rearrange("p b h -> p (b h)"),
                                        start=True, stop=True)], r=['lf', 'triinc'], w=['ps0'])
        t.mms([lambda: nc.tensor.matmul(self.psb(1)[:, 0:n], lhsT=self.allones[:], rhs=lf.rearrange("p b h -> p (b h)"),
                                        start=True, stop=True)], r=['lf', 'allones'], w=['ps1'])
        off = self.offt
        if carry_in is None:
            t.op('dve', lambda: nc.vector.memset(off[:, 0, :], 0.0), w=['off'])
        else:
            t.op('dve', lambda: nc.vector.tensor_copy(out=off[:, 0, :], in_=carry_in), r=['carry'], w=['off'])
        for b in range(1, nb + 1):
            t.op('dve', lambda b=b: nc.vector.tensor_tensor(out=off[:, b, :], in0=self.psb(1)[:, (b - 1) * H:b * H],
                                                         in1=off[:, b - 1, :], op=ALU.add), r=['ps1', 'off'], w=['off'])
        t.op('dve', lambda: nc.vector.tensor_tensor(out=cdst, in0=self.psb(0)[:, 0:n].rearrange("p (b h) -> p b h", h=H),
                                                    in1=off[:, 0:nb, :], op=ALU.add), r=['ps0', 'off'], w=['ctm'])
        if update_carry:
            t.op('dve', lambda: nc.vector.tensor_copy(out=self.carry[:], in_=off[:, nb, :]), r=['off'], w=['carry'])

    def logsig(self, psrc, nb, bfb, lf):
        nc, t = self.nc, self.t
        n = nb * H
        t.op('dve', lambda: nc.vector.tensor_tensor(out=lf, in0=psrc.rearrange("p (b h) -> p b h", h=H),
                                                    in1=bc(bfb[:].unsqueeze(1), [128, nb, H]), op=ALU.add),
             r=['ps5', 'bfb'], w=['lf'])
        t.op('act', lambda: nc.scalar.activation(out=lf, in_=lf, func=AF.Exp, scale=-1.0), r=['lf'], w=['lf'])
        t.op('act', lambda: nc.scalar.activation(out=lf, in_=lf, func=AF.Ln, bias=self.one_t[:], scale=1.0),
             r=['lf', 'c_one'], w=['lf'])
        t.op('dve', lambda: nc.vector.tensor_scalar(out=lf, in0=lf, scalar1=-1.0, scalar2=None, op0=ALU.mult),
             r=['lf'], w=['lf'])

    def pool_windows(self, g, ubuf, tmpa, tmpb, dT, w, first):
        nc, t = self.nc, self.t
        W = 2 ** (g + 1)
        n = 16 + w
        src, srck = ubuf, 'ubuf'
        bufs = [(tmpa, 'tmpa'), (tmpb, 'tmpb')]
        sh = 1
        k = 0
        while sh < W:
            dst, dstk = bufs[k % 2]
            t.op('dve', lambda src=src, dst=dst, sh=sh: nc.vector.tensor_tensor(
                out=dst[:, 2 * sh:n], in0=src[:, 2 * sh:n], in1=src[:, sh:n - sh], op=ALU.add),
                r=[srck], w=[dstk])
            src, srck = dst, dstk
            sh *= 2
            k += 1
        t.op('dve', lambda src=src: nc.vector.scalar_tensor_tensor(out=dT[:, 0:w], in0=src[:, 16:n], scalar=1.0 / W,
                                                                  in1=ubuf[:, 16:n], op0=ALU.mult, op1=ALU.subtract),
             r=[srck, 'ubuf'], w=['dT'])
        if first and W > 1:
            t.op('dve', lambda src=src: nc.vector.tensor_tensor(out=tmpa[:, 0:W - 1], in0=src[:, 16:16 + W - 1],
                                                               in1=self.invc[:, 0:W - 1], op=ALU.mult),
                 r=[srck, 'c_invc', 'tmpa', 'tmpb'], w=['tmpa', 'tmpb'])
            t.op('dve', lambda: nc.vector.tensor_tensor(out=dT[:, 0:W - 1], in0=tmpa[:, 0:W - 1], in1=ubuf[:, 16:16 + W - 1],
                                                        op=ALU.subtract), r=['tmpa', 'tmpb', 'ubuf'], w=['dT'])

    def even_prompt(self, i, hf, post=None):
        nc, t = self.nc, self.t
        NBP, PL, TW = self.NBP, self.PL, self.TW
        gb0 = hf * NBP
        pos0 = hf * PL
        with ExitStack() as es:
            sb = lambda n, sh, dt=F32: self.sb(es, n, sh, dt)
            QT = sb("QT", [128, 4, PL], BF16)
            poolT = sb("poolT", [128, 4, PL], BF16)
            NPR = max(1, NBP // 2)
            bias_all = sb("bias_all", [128, self.NBS, NPR, H])
            crefb = sb("crefb", [128, NBP, H])
            lf = sb("lf", [128, NBP, H])
            self.offt = sb("offt", [128, max(NBP, self.NBC + 1) + 1, H])
            bfb = sb("bfb", [128, H])
            pw = sb("pw", [128, 4, 128], BF16)
            pwst = sb("pwst", [128, 4, 128])
            pscale = sb("pscale", [128, 4])
            wo = None
            if hf == 0:
                t.op('pool', lambda: nc.gpsimd.memset(self.hist[:], 0.0), w=['hist'])
            with ExitStack() as e1:
                sb1 = lambda n, sh, dt=F32: self.sb(e1, n, sh, dt)
                stage = sb1("stage", [128, 8, 512])
                kst = [sb1("kst%d" % k, [128, 512]) for k in range(2)]
                vst = [sb1("vst%d" % k, [128, 512]) for k in range(2)]
                ubuf = sb1("ubuf", [128, 16 + TW])
                tmpa = sb1("tmpa", [128, 16 + TW])
                tmpb = sb1("tmpb", [128, 16 + TW])
                dT = sb1("dT", [128, TW], BF16)
                ust = sb1("ust", [128, 512])
                tiles = [(c0, TW) for c0 in range(0, PL, TW)]
                bank = [0]

                def nb():
                    bank[0] = (bank[0] + 1) % 4
                    return bank[0]

                def hk(c0, w):
                    return ['hT%d' % b for b in range(c0 // 128, (c0 + w) // 128)]

                for which, col0 in (("q", 0), ("k", 512)):
                    wt = self.load_w_bf16(e1, "w_" + which, self.even_w_in[:, col0:col0 + 512], 512, stage)
                    for (c0, w) in tiles:
                        for p in range(4):
                            bk = nb()
                            t.mms([lambda kc=kc, p=p, bk=bk, wt=wt, c0=c0, w=w: nc.tensor.matmul(
                                self.psb(bk)[:, 0:w], lhsT=wt[:, kc, p * 128:(p + 1) * 128], rhs=self.hT[:, kc, c0:c0 + w],
                                start=(kc == 0), stop=(kc == 7)) for kc in range(8)],
                                r=["w_" + which] + hk(c0, w), w=['ps%d' % bk])
                            dst = QT[:, p, c0:c0 + w] if which == "q" else self.KT[:, p, pos0 + c0:pos0 + c0 + w]
                            t.op('act', lambda dst=dst, bk=bk, w=w: nc.scalar.copy(out=dst, in_=self.psb(bk)[:, 0:w]),
                                 r=['ps%d' % bk], w=['QT' if which == "q" else 'KT'])
                self.chk("e_qk")
                t.dma('sp', out=bfb[:], in_=self.even_b_f.partition_broadcast(128), w=['bfb'], sem='misc')
                t.dma('sp', out=pwst[:], in_=self.pool_w.rearrange("g c d -> c g d"), w=['pwst'], sem='misc2')
                t.op('pool', lambda: nc.gpsimd.tensor_copy(out=pw[:], in_=pwst[:]), r=['pwst'], w=['pw'])
                for g in range(4):
                    t.dma('sp', out=pscale[:, g:g + 1], in_=self.pool_scale[g * 128:(g + 1) * 128].rearrange("(p o) -> p o", o=1),
                          w=['pscale'], sem='misc3_%d' % g)
                wkt = self.load_w_bf16(e1, "w_k2", self.even_w_in[:, 512:1024], 512, stage)
                wvt = self.load_w_bf16(e1, "w_v", self.even_w_in[:, 1024:1536], 512, stage)
                wft = self.load_w_bf16(e1, "w_f", self.even_w_in[:, 1536:1544], H, stage)
                wut = self.load_w_bf16(e1, "w_u", self.even_w_in[:, 1544:2056], 512, stage)
                for b in range(NBP):
                    t.mms([lambda kc=kc, b=b: nc.tensor.matmul(
                        self.psb(5)[:, b * H:(b + 1) * H], lhsT=self.hT[:, kc, b * 128:(b + 1) * 128], rhs=wft[:, kc, :],
                        start=(kc == 0), stop=(kc == 7)) for kc in range(8)], r=['w_f', 'hT%d' % b], w=['ps5'])
                self.logsig(self.psb(5)[:, 0:NBP * H], NBP, bfb, lf[:])
                t.dma('sp', out=self.nlf_p[i, pos0:pos0 + PL, :].rearrange("(b p) h -> p b h", p=128), in_=lf[:],
                      r=['lf'], sem='olf')
                self.chk("e_f")

                def kv_item(b, which):
                    wt, st, dstd = (wkt, kst, self.nk_p) if which == "k" else (wvt, vst, self.nv_p)
                    bk = nb()
                    t.mms([lambda kc=kc: nc.tensor.matmul(
                        self.psb(bk), lhsT=self.hT[:, kc, b * 128:(b + 1) * 128], rhs=wt[:, kc, :],
                        start=(kc == 0), stop=(kc == 7)) for kc in range(8)],
                        r=["w_k2" if which == "k" else "w_v", 'hT%d' % b], w=['ps%d' % bk])
                    k2 = b % 2
                    t.op('act', lambda: nc.scalar.copy(out=st[k2][:], in_=self.psb(bk)),
                         r=['ps%d' % bk], w=['%sst%d' % (which, k2)])
                    if which == "v":
                        t.op('dve', lambda: nc.vector.tensor_copy(
                            out=self.Vaug[:, gb0 + b, :, 0:64], in_=vst[k2][:].rearrange("p (h d) -> p h d", h=H)),
                            r=['vst%d' % k2], w=['Vaug'])
                    t.dma('sp', out=dstd[i, pos0 + b * 128:pos0 + (b + 1) * 128, :], in_=st[k2][:],
                          r=['%sst%d' % (which, k2)], sem='o%s%d' % (which, k2))

                def pool_item(ti, c0, w, g):
                    bk = nb()
                    t.mms([lambda kc=kc: nc.tensor.matmul(
                        self.psb(bk)[:, 0:w], lhsT=wut[:, kc, g * 128:(g + 1) * 128], rhs=self.hT[:, kc, c0:c0 + w],
                        start=(kc == 0), stop=(kc == 7)) for kc in range(8)], r=['w_u'] + hk(c0, w), w=['ps%d' % bk])
                    t.op('pool', lambda: nc.gpsimd.tensor_copy(out=ubuf[:, 0:16], in_=self.hist[:, g, :]),
                         r=['hist', 'tmpa', 'tmpb', 'dT'], w=['ubuf'])
                    t.op('act', lambda: nc.scalar.copy(out=ubuf[:, 16:16 + w], in_=self.psb(bk)[:, 0:w]),
                         r=['ps%d' % bk], w=['ubuf'])
                    t.op('pool', lambda: nc.gpsimd.tensor_copy(out=self.hist[:, g, 1:16], in_=ubuf[:, w + 1:w + 16]),
                         r=['ubuf'], w=['hist'])
                    self.pool_windows(g, ubuf, tmpa, tmpb, dT, w, first=(hf == 0 and ti == 0))
                    bk2 = nb()
                    t.mms([lambda: nc.tensor.matmul(self.psb(bk2)[:, 0:w], lhsT=pw[:, g, :], rhs=dT[:, 0:w],
                                                    start=True, stop=True)], r=['pw', 'dT'], w=['ps%d' % bk2])
                    t.op('act', lambda: nc.scalar.activation(
                        out=poolT[:, g, c0:c0 + w], in_=self.psb(bk2)[:, 0:w], func=AF.Copy, scale=pscale[:, g:g + 1]),
                        r=['ps%d' % bk2, 'pscale'], w=['poolT'])

                kv_list = [(b, which) for b in range(NBP) for which in ("k", "v")]
                pool_list = [(ti, c0, w, g) for ti, (c0, w) in enumerate(tiles) for g in range(4)]
                ratio = -(-len(kv_list) // len(pool_list))
                ki = 0
                for pi, it in enumerate(pool_list):
                    if pi == min(2, len(pool_list) - 1):
                        self.cumsum_blocks(lf[:], NBP, self.ctm[:, gb0:gb0 + NBP, :], None if hf == 0 else self.carry[:])
                    pool_item(*it)
                    for _ in range(ratio):
                        if ki < len(kv_list):
                            kv_item(*kv_list[ki])
                            ki += 1
                while ki < len(kv_list):
                    kv_item(*kv_list[ki])
                    ki += 1
                self.chk("e_pool")
                if hf == self.NH - 1:
                    b = NBP - 1
                    bk = nb()
                    t.mms([lambda kc=kc, bk=bk: nc.tensor.matmul(
                        self.psb(bk), lhsT=self.hT[:, kc, b * 128:(b + 1) * 128], rhs=wut[:, kc, :],
                        start=(kc == 0), stop=(kc == 7)) for kc in range(8)], r=['w_u', 'hT%d' % b], w=['ps%d' % bk])
                    t.op('act', lambda bk=bk: nc.scalar.copy(out=ust[:], in_=self.psb(bk)), r=['ps%d' % bk], w=['ust'])
                    t.dma('sp', out=self.npool_p[i], in_=ust[113:128, :], r=['ust'], sem='opool')
                t.barrier()
            with ExitStack() as e2:
                sb2 = lambda n, sh, dt=F32: self.sb(e2, n, sh, dt)
                stage = sb2("stage2", [128, 8, D])
                wo = self.load_w_bf16(e2, "w_o", self.even_w_out, D, stage)
                Pt = [sb2("Pt%d" % k, [128, 512], BF16) for k in range(3)]
                att = sb2("att", [128, 4, 512], BF16)
                attT = sb2("attT", [128, 4, 512], BF16)
                rec = sb2("rec", [128, 4])
                nkb = gb0 + NBP
                assert NBP % 2 == 0
                t.mms([lambda: nc.tensor.matmul(self.psb(5)[:, 0:NBP * H], lhsT=self.sel0[:],
                                                rhs=self.ctm[:, gb0:gb0 + NBP, :].rearrange("p b h -> p (b h)"),
                                                start=True, stop=True)], r=['ctm', 'sel0'], w=['ps5'])
                t.op('dve', lambda: nc.vector.tensor_copy(out=crefb[:].rearrange("p b h -> p (b h)"), in_=self.psb(5)[:, 0:NBP * H]),
                     r=['ps5'], w=['crefb'])
                cpair = crefb[:].rearrange("p (q two) h -> p q two h", two=2)[:, :, 1, :]
                for j in range(nkb):
                    t.op('dve', lambda j=j: nc.vector.tensor_tensor(
                        out=bias_all[:, j, :, :], in0=cpair, in1=bc(self.ctm[:, j:j + 1, :], [128, NPR, H]),
                        op=ALU.subtract), r=['crefb', 'ctm'], w=['bias'])
                self.chk("e_bias")
                pend_e = []
                for (c0, w) in [(c0, TW) for c0 in range(0, PL, TW)]:
                    s0 = c0 // 128
                    nsb = w // 128
                    units = [(h, j) for h in range(H) for j in range(gb0 + s0 + nsb)]
                    LA = 2
                    jl = gb0 + s0 + nsb - 1

                    def qk_exp(u):
                        h, j = units[u]
                        p, bp = h // 2, 64 * (h % 2)
                        sbk = u % 3
                        smin = max(0, j - (gb0 + s0))
                        t.mms([lambda: nc.tensor.matmul(
                            self.psb(sbk)[:, smin * 128:nsb * 128], lhsT=self.KT[bp:bp + 64, p, j * 128:(j + 1) * 128],
                            rhs=QT[bp:bp + 64, p, c0 + smin * 128:c0 + nsb * 128], start=True, stop=True)],
                            r=['KT', 'QT'], w=['ps%d' % sbk])
                        for pp in range(smin // 2, nsb // 2):
                            lo, hi = max(smin, 2 * pp), 2 * pp + 2
                            t.op('act', lambda lo=lo, hi=hi, pp=pp: nc.scalar.activation(
                                out=Pt[sbk][:, lo * 128:hi * 128], in_=self.psb(sbk)[:, lo * 128:hi * 128],
                                func=AF.Exp, bias=bias_all[:, j, s0 // 2 + pp, h:h + 1], scale=0.125),
                                r=['ps%d' % sbk, 'bias'], w=['Pt%d_%d' % (sbk, si) for si in range(lo, hi)])
                        sd = j - (gb0 + s0)
                        if sd >= 0:
                            t.op('pool', lambda: nc.gpsimd.tensor_tensor(
                                out=Pt[sbk][:, sd * 128:(sd + 1) * 128], in0=Pt[sbk][:, sd * 128:(sd + 1) * 128],
                                in1=self.trimask[:], op=ALU.mult), r=['Pt%d_%d' % (sbk, sd), 'trimask'], w=['Pt%d_%d' % (sbk, sd)])

                    def pv(u):
                        h, j = units[u]
                        sbk = u % 3
                        ob = 3 + (h % 2)
                        smin = max(0, j - (gb0 + s0))
                        fns = []
                        for si in range(smin, nsb):
                            fns.append(lambda si=si, st=(j == 0 and si == smin): nc.tensor.matmul(
                                self.psb(ob)[:, si * 65:(si + 1) * 65], lhsT=Pt[sbk][:, si * 128:(si + 1) * 128],
                                rhs=self.Vaug[:, j, h, :], start=st, stop=(j == gb0 + s0 + si), skip_group_check=True))
                        t.mms(fns, r=['Pt%d_%d' % (sbk, si) for si in range(smin, nsb)] + ['Vaug', 'vones'], w=['ps%d' % ob])
                        if j == jl:
                            Ov = self.psb(ob)[:, 0:nsb * 65].rearrange("p (s e) -> p s e", e=65)
                            t.op('dve', lambda: nc.vector.reciprocal(out=rec[:, 0:nsb].unsqueeze(2), in_=Ov[:, :, 64:65]),
                                 r=['ps%d' % ob], w=['rec'])
                            t.op('dve', lambda: nc.vector.tensor_tensor(
                                out=att[:, 0:nsb, h * 64:(h + 1) * 64], in0=Ov[:, :, 0:64],
                                in1=bc(rec[:, 0:nsb].unsqueeze(2), [128, nsb, 64]), op=ALU.mult),
                                r=['ps%d' % ob, 'rec'], w=['att%d' % h])

                    for u in range(len(units) + LA):
                        if u < len(units):
                            qk_exp(u)
                        if u - LA >= 0:
                            pv(u - LA)
                        if u == len(units) // 2 and post is not None and pend_e:
                            post(pend_e)
                            pend_e = []
                    self.chk("e_att")
                    for si in range(nsb):
                        bank = 6
                        self.tq += 1
                        pst = self.psb16(bank)
                        t.mms([lambda c=c, si=si, pst=pst: nc.tensor.transpose(
                            out=pst[:, c * 128:(c + 1) * 128], in_=att[:, si, c * 128:(c + 1) * 128], identity=self.identb[:])
                            for c in range(4)], r=['att%d' % h for h in range(H)] + ['identb'], w=['ps%d' % bank])
                        t.op('act', lambda si=si, pst=pst: nc.scalar.copy(
                            out=attT[:, :, si * 128:(si + 1) * 128], in_=pst[:, 0:512].rearrange("p (c t) -> p c t", c=4)),
                            r=['ps%d' % bank], w=['attT%d' % si])
                        b = s0 + si
                        for dh in range(2):
                            yb = 5 if dh == 0 else 7
                            fns = []
                            for c in range(8):
                                lh = attT[:, c, si * 128:(si + 1) * 128] if c < 4 else poolT[:, c - 4, c0 + si * 128:c0 + (si + 1) * 128]
                                fns.append(lambda c=c, lh=lh, dh=dh, yb=yb: nc.tensor.matmul(
                                    self.psb(yb), lhsT=lh, rhs=wo[:, c, dh * 512:(dh + 1) * 512], start=(c == 0), stop=(c == 7)))
                            t.mms(fns, r=['attT%d' % si, 'poolT', 'w_o'], w=['ps%d' % yb])
                            xs = self.x[:, b, dh * 512:(dh + 1) * 512]
                            t.op('dve', lambda xs=xs, yb=yb: nc.vector.tensor_tensor(out=xs, in0=self.psb(yb), in1=xs, op=ALU.add),
                                 r=['ps%d' % yb, 'x%d_%d' % (b, dh)], w=['x%d_%d' % (b, dh)])
                    if post is not None:
                        if pend_e:
                            post(pend_e)
                        pend_e = list(range(s0, s0 + nsb))
                if post is not None and pend_e:
                    post(pend_e)
                t.barrier()

    def even_sample(self, post=None):
        nc, t = self.nc, self.t
        PL, NBC, NS, T = self.PL, self.NBC, self.NS, self.T
        SBk = self.SB
        hc0 = PL
        with ExitStack() as es:
            sb = lambda n, sh, dt=F32: self.sb(es, n, sh, dt)
            wq = self.load_w_bf16(es, "w_q", self.even_w_in[:, 0:512], 512, None)
            wk = self.load_w_bf16(es, "w_k", self.even_w_in[:, 512:1024], 512, None)
            wv = self.load_w_bf16(es, "w_v", self.even_w_in[:, 1024:1536], 512, None)
            wf = self.load_w_bf16(es, "w_f", self.even_w_in[:, 1536:1544], H, None)
            wu = self.load_w_bf16(es, "w_u", self.even_w_in[:, 1544:2056], 512, None)
            wo = self.load_w_bf16(es, "w_o", self.even_w_out, D, None)
            QTs = sb("QTs", [128, 4, 128], BF16)
            KTs = sb("KTs", [128, 4, 128], BF16)
            KTc = sb("KTc", [128, 4, self.P], BF16)
            Vc = sb("Vc", [128, NBC + 1, H, 65], BF16)
            cbf = sb("cbf", [128, NBC, 512], BF16)
            lfa = sb("lfa", [128, NBC + 1, H])
            call = sb("call", [128, NBC + 1, H])
            self.offt = sb("offt_s", [128, NBC + 3, H])
            bias = sb("bias_s", [128, NBC + 1, H])
            crefb = sb("crefb_s", [128, H])
            bfb = sb("bfb_s", [128, H])
            Pt = [sb("Pts%d" % k, [128, 32], BF16) for k in range(3)]
            atts = sb("atts", [32, 512], BF16)
            mixT = sb("mixTs", [128, 8, 128], BF16)
            rec = sb("recs", [32, 1])
            kst = sb("ksts", [32, 512])
            vst = sb("vsts", [32, 512])
            ust = sb("usts", [32, 512])
            hst = sb("hsts", [16, 512])
            uTs = sb("uTs", [128, 4, 128])
            dTs = sb("dTs", [128, 4, 128], BF16)
            ubuf = sb("ubufs", [128, 48])
            tmpa = sb("tmpas", [128, 48])
            tmpb = sb("tmpbs", [128, 48])
            dtmp = sb("dtmps", [128, 32], BF16)
            pw = sb("pws", [128, 4, 128], BF16)
            pwst = sb("pwsts", [128, 4, 128])
            pscale = sb("pscales", [128, 4])
            t.dma('sp', out=bfb[:], in_=self.even_b_f.partition_broadcast(128), w=['bfb'], sem='misc')
            t.dma('sp', out=pwst[:], in_=self.pool_w.rearrange("g c d -> c g d"), w=['pwst'], sem='misc2')
            t.op('pool', lambda: nc.gpsimd.tensor_copy(out=pw[:], in_=pwst[:]), r=['pwst'], w=['pw'])
            for g in range(4):
                t.dma('sp', out=pscale[:, g:g + 1], in_=self.pool_scale[g * 128:(g + 1) * 128].rearrange("(p o) -> p o", o=1),
                      w=['pscale'], sem='misc3_%d' % g)
            t.op('pool', lambda: nc.gpsimd.memset(Vc[:, :, :, 64:65], 1.0), w=['Vc1'])
            hk = ['hT%d' % SBk]
            for which, wt, dst in (("q", wq, QTs), ("k", wk, KTs), ("u", wu, uTs)):
                for p in range(4):
                    bk = p
                    t.mms([lambda kc=kc, p=p, bk=bk, wt=wt: nc.tensor.matmul(
                        self.psb(bk)[:, 0:128], lhsT=wt[:, kc, p * 128:(p + 1) * 128], rhs=self.hT[:, kc, hc0:hc0 + 128],
                        start=(kc == 0), stop=(kc == 7)) for kc in range(8)], r=['w_q', 'w_k', 'w_u'] + hk,
                        w=['ps%d' % bk])
                    t.op('act', lambda dst=dst, p=p, bk=bk: nc.scalar.copy(out=dst[:, p, :], in_=self.psb(bk)[:, 0:128]),
                         r=['ps%d' % bk], w=['s_' + which])
            oq = 0
            sq = 0
            for si in range(NS):
                cs = hc0 + si * T
                for which, wt, st, dstd in (("k", wk, kst, self.nk_s), ("v", wv, vst, self.nv_s), ("u", wu, ust, None)):
                    bk = 4
                    t.mms([lambda kc=kc, wt=wt: nc.tensor.matmul(
                        self.psb(4)[0:32, :], lhsT=self.hT[:, kc, cs:cs + T], rhs=wt[:, kc, :],
                        start=(kc == 0), stop=(kc == 7)) for kc in range(8)], r=['w_k', 'w_v', 'w_u'] + hk, w=['ps4'])
                    t.op('act', lambda st=st: nc.scalar.copy(out=st[:], in_=self.psb(4)[0:32, :]), r=['ps4'], w=['st_' + which])
                    if which == "v":
                        t.op('pool', lambda: nc.gpsimd.tensor_copy(out=Vc[0:32, NBC, :, 0:64],
                                                                   in_=vst[:].rearrange("p (h d) -> p h d", h=H)),
                             r=['st_v'], w=['Vcn'])
                    if dstd is not None:
                        t.dma('sp', out=dstd[si], in_=st[:], r=['st_' + which], sem='os_' + which)
                    else:
                        t.dma('sp', out=self.npool_s[si], in_=st[17:32, :], r=['st_' + which], sem='os_' + which)
                t.dma('pool', out=cbf[:], in_=self.ck[si].rearrange("(b p) c -> p b c", p=128), w=['cbf'], sem='cin')
                for b in range(NBC):
                    bank = 6
                    self.tq += 1
                    pst = self.psb16(bank)
                    t.mms([lambda c=c, b=b, pst=pst: nc.tensor.transpose(out=pst[:, c * 128:(c + 1) * 128],
                                                                      in_=cbf[:, b, c * 128:(c + 1) * 128], identity=self.identb[:])
                           for c in range(4)], r=['cbf', 'identb'], w=['ps%d' % bank])
                    t.op('act', lambda b=b, pst=pst: nc.scalar.copy(out=KTc[:, :, b * 128:(b + 1) * 128],
                                                                  in_=pst[:, 0:512].rearrange("p (c t) -> p c t", c=4)),
                         r=['ps%d' % bank], w=['KTc'])
                for b in range(NBC):
                    t.dma('pool', out=Vc[:, b, :, 0:64], in_=self.cv[si][b * 128:(b + 1) * 128, :].rearrange("p (h d) -> p h d", h=H),
                          w=['Vcc'], sem='cinv')
                t.op('dve', lambda: nc.vector.memset(lfa[:], 0.0), w=['lf'])
                t.dma('sp', out=lfa[:, 0:NBC, :], in_=self.clf[si].rearrange("(b p) h -> p b h", p=128), w=['lf'], sem='cin2')
                t.mms([lambda kc=kc: nc.tensor.matmul(self.psb(5)[0:32, 0:H], lhsT=self.hT[:, kc, cs:cs + T], rhs=wf[:, kc, :],
                                                      start=(kc == 0), stop=(kc == 7)) for kc in range(8)], r=['w_f'] + hk, w=['ps5'])
                nc_ = nc
                t.op('dve', lambda: nc_.vector.tensor_tensor(out=lfa[0:32, NBC, :], in0=self.psb(5)[0:32, 0:H], in1=bfb[0:32, :], op=ALU.add),
                     r=['ps5', 'bfb', 'lf'], w=['lf'])
                t.op('act', lambda: nc_.scalar.activation(out=lfa[0:32, NBC, :], in_=lfa[0:32, NBC, :], func=AF.Exp, scale=-1.0),
                     r=['lf'], w=['lf'])
                t.op('act', lambda: nc_.scalar.activation(out=lfa[0:32, NBC, :], in_=lfa[0:32, NBC, :], func=AF.Ln,
                                                         bias=self.one_t[0:32, :], scale=1.0), r=['lf', 'c_one'], w=['lf'])
                t.op('dve', lambda: nc_.vector.tensor_scalar(out=lfa[0:32, NBC, :], in0=lfa[0:32, NBC, :], scalar1=-1.0, scalar2=None,
                                                            op0=ALU.mult), r=['lf'], w=['lf'])
                t.dma('sp', out=self.nlf_s[si], in_=lfa[0:32, NBC, :], r=['lf'], sem='os_lf')
                self.cumsum_blocks(lfa[:], NBC + 1, call[:], None, update_carry=False)
                t.mms([lambda: nc.tensor.matmul(self.psb(5)[:, 0:H], lhsT=self.sel16[:], rhs=call[:, NBC, :], start=True, stop=True)],
                      r=['ctm', 'sel16'], w=['ps5'])
                t.op('dve', lambda: nc.vector.tensor_copy(out=crefb[:], in_=self.psb(5)[:, 0:H]), r=['ps5'], w=['crefb'])
                t.op('dve', lambda: nc.vector.tensor_tensor(out=bias[:], in0=bc(crefb[:].unsqueeze(1), [128, NBC + 1, H]), in1=call[:],
                                                            op=ALU.subtract), r=['crefb', 'ctm'], w=['bias'])
                units = [(h, j) for h in range(H) for j in range(NBC + 1)]
                LA = 2

                def qk_exp(u):
                    h, j = units[u]
                    p, bp = h // 2, 64 * (h % 2)
                    sbk = u % 3
                    nk = 128 if j < NBC else T
                    lh = KTc[bp:bp + 64, p, j * 128:(j + 1) * 128] if j < NBC else KTs[bp:bp + 64, p, si * T:(si + 1) * T]
                    t.mms([lambda: nc.tensor.matmul(
                        self.psb(sbk)[0:nk, 0:T], lhsT=lh, rhs=QTs[bp:bp + 64, p, si * T:(si + 1) * T], start=True, stop=True)],
                        r=['KTc', 's_k', 's_q'], w=['ps%d' % sbk])
                    t.op('act', lambda: nc.scalar.activation(
                        out=Pt[sbk][0:nk, :], in_=self.psb(sbk)[0:nk, 0:T], func=AF.Exp, bias=bias[0:nk, j, h:h + 1], scale=0.125),
                        r=['ps%d' % sbk, 'bias'], w=['Pts%d' % sbk])
                    if j == NBC:
                        t.op('pool', lambda: nc.gpsimd.tensor_tensor(out=Pt[sbk][0:T, :], in0=Pt[sbk][0:T, :],
                                                                     in1=self.trimask[0:T, 0:T], op=ALU.mult),
                             r=['Pts%d' % sbk, 'trimask'], w=['Pts%d' % sbk])

                def pv(u):
                    h, j = units[u]
                    sbk = u % 3
                    ob = 3 + (h % 2)
                    nk = 128 if j < NBC else T
                    t.mms([lambda: nc.tensor.matmul(
                        self.psb(ob)[0:32, 0:65], lhsT=Pt[sbk][0:nk, :], rhs=Vc[0:nk, j, h, :], start=(j == 0), stop=(j == NBC))],
                        r=['Pts%d' % sbk, 'Vcc', 'Vcn', 'Vc1'], w=['ps%d' % ob])
                    if j == NBC:
                        t.op('dve', lambda: nc.vector.reciprocal(out=rec[:], in_=self.psb(ob)[0:32, 64:65]), r=['ps%d' % ob], w=['recs'])
                        t.op('dve', lambda: nc.vector.tensor_scalar(out=atts[:, h * 64:(h + 1) * 64], in0=self.psb(ob)[0:32, 0:64],
                                                                    scalar1=rec[:, 0:1], scalar2=None, op0=ALU.mult),
                             r=['ps%d' % ob, 'recs'], w=['atts'])

                for u in range(len(units) + LA):
                    if u < len(units):
                        qk_exp(u)
                    if u - LA >= 0:
                        pv(u - LA)
                bank = 6
                self.tq += 1
                pst = self.psb16(bank)
                t.mms([lambda c=c, pst=pst: nc.tensor.transpose(out=pst[:, c * 32:(c + 1) * 32], in_=atts[:, c * 128:(c + 1) * 128],
                                                                identity=self.identb[0:32, 0:32]) for c in range(4)],
                      r=['atts', 'identb'], w=['ps%d' % bank])
                t.op('act', lambda pst=pst: nc.scalar.copy(out=mixT[:, 0:4, si * T:(si + 1) * T],
                                                          in_=pst[:, 0:128].rearrange("p (c t) -> p c t", c=4)),
                     r=['ps%d' % bank], w=['mixTs'])
                t.dma('sp', out=hst[0:15, :], in_=self.spool[si], w=['hst'], sem='cin3')
                for g in range(4):
                    t.mms([lambda g=g: nc.tensor.transpose(out=self.psb(4)[:, 0:15], in_=hst[0:15, g * 128:(g + 1) * 128],
                                                           identity=self.identf[0:15, 0:15])], r=['hst', 'identf'], w=['ps4'])
                    t.op('act', lambda: nc.scalar.copy(out=ubuf[:, 1:16], in_=self.psb(4)[:, 0:15]),
                         r=['ps4', 'tmpa', 'tmpb', 'dT'], w=['ubuf'])
                    t.op('pool', lambda g=g: nc.gpsimd.tensor_copy(out=ubuf[:, 16:16 + T], in_=uTs[:, g, si * T:(si + 1) * T]),
                         r=['s_u'], w=['ubuf'])
                    self.pool_windows(g, ubuf, tmpa, tmpb, dtmp, T, first=False)
                    t.op('pool', lambda g=g: nc.gpsimd.tensor_copy(out=dTs[:, g, si * T:(si + 1) * T], in_=dtmp[:]),
                         r=['dT'], w=['dTs'])
            for g in range(4):
                t.mms([lambda g=g: nc.tensor.matmul(self.psb(g)[:, 0:128], lhsT=pw[:, g, :], rhs=dTs[:, g, :], start=True, stop=True)],
                      r=['pw', 'dTs'], w=['ps%d' % g])
                t.op('act', lambda g=g: nc.scalar.activation(out=mixT[:, 4 + g, :], in_=self.psb(g)[:, 0:128], func=AF.Copy,
                                                             scale=pscale[:, g:g + 1]), r=['ps%d' % g, 'pscale'], w=['mixTs'])
            for dh in range(2):
                t.mms([lambda c=c, dh=dh: nc.tensor.matmul(self.psb(5), lhsT=mixT[:, c, :], rhs=wo[:, c, dh * 512:(dh + 1) * 512],
                                                          start=(c == 0), stop=(c == 7)) for c in range(8)],
                      r=['mixTs', 'w_o'], w=['ps5'])
                xs = self.x[:, SBk, dh * 512:(dh + 1) * 512]
                t.op('dve', lambda xs=xs: nc.vector.tensor_tensor(out=xs, in0=self.psb(5), in1=xs, op=ALU.add),
                     r=['ps5', 'x%d_%d' % (SBk, dh)], w=['x%d_%d' % (SBk, dh)])
            if post is not None:
                post([SBk])
            t.barrier()

    def odd(self, i, hf, blocks, post=None, gk=None):
        nc, t = self.nc, self.t
        T, NS = self.T, self.NS
        with ExitStack() as es:
            sb = lambda n, sh, dt=F32: self.sb(es, n, sh, dt)
            w_in = [sb("wsg_in%d" % q, [128, 8, 512], BF16) for q in range(4)]
            wo = [sb("wsg_o%d" % dh, [128, 8, 512], BF16) for dh in range(2)]
            wsn = sb("wsn", [128, 8, 128])
            wsT = sb("wsT", [128, 8, 128], BF16)
            wsTs = sb("wsTs", [128, 8, 128], BF16)
            bsn = sb("bsn", [8, 128])
            bsT = sb("bsT", [128, 8])
            bsTs = sb("bsTs", [128, 8])
            gsg = sb("gsg", [128, D])
            t.dma('sp', out=gsg[:], in_=self.sgu_norm_g.partition_broadcast(128), w=['gsg'], sem='misc')
            t.dma('sp', out=wsn[:], in_=self.sgu_w_s.rearrange("g t s -> t g s"), w=['wsn'], sem='misc2')
            t.dma('sp', out=bsn[:], in_=self.sgu_b_s, w=['bsn'], sem='misc4')
            with ExitStack() as e0:
                first = not (hasattr(self, 'scr') and 'wsg_in0' in self.scr)
                stage = self.sb(e0, "stage_o", [128, 8, 512]) if first else None
                for q in range(4):
                    self.load_w_bf16(es, "wsg_in%d" % q, self.sgu_w_in[:, q * 512:(q + 1) * 512], 512, stage, wt=w_in[q])
                for dh in range(2):
                    self.load_w_bf16(es, "wsg_o%d" % dh, self.sgu_w_out[:, dh * 512:(dh + 1) * 512], 512, stage, wt=wo[dh])
                if first:
                    t.barrier()
            zu = [sb("zu%d" % k, [128, D]) for k in range(2)]
            zv = [sb("zv%d" % k, [128, D]) for k in range(2)]
            zvn = [sb("zvn%d" % k, [128, D], BF16) for k in range(2)]
            zvf = sb("zvf", [128, D])
            gated = [sb("gated%d" % k, [128, D], BF16) for k in range(2)]
            gT = [sb("gT%d" % k, [128, 8, 128], BF16) for k in range(2)]
            rs = sb("rs_o", [128, 4])
            rso = sb("rso_o", [128, 4])
            t.op('pool', lambda: nc.gpsimd.memset(wsn[0:64, :, 64:128], 0.0), r=['wsn'], w=['wsn'])
            for g in range(8):
                t.mms([lambda g=g: nc.tensor.transpose(out=self.psb(g % 4)[:, 0:128], in_=wsn[:, g, :], identity=self.identf[:])],
                      r=['wsn', 'identf'], w=['ps%d' % (g % 4)])
                t.op('act', lambda g=g: nc.scalar.copy(out=wsT[:, g, :], in_=self.psb(g % 4)[:, 0:128]), r=['ps%d' % (g % 4)], w=['wsT'])
            t.mms([lambda: nc.tensor.transpose(out=self.psb(4)[:, 0:8], in_=bsn[:], identity=self.identf[0:8, 0:8])],
                  r=['bsn', 'identf'], w=['ps4'])
            t.op('act', lambda: nc.scalar.copy(out=bsT[:], in_=self.psb(4)[:, 0:8]), r=['ps4'], w=['bsT'])
            if self.SB in blocks:
                t.op('pool', lambda: nc.gpsimd.memset(wsTs[:], 0.0), w=['wsTs'])
                for si in range(NS):
                    t.dma('sp', out=wsTs[si * T:(si + 1) * T, :, si * T:(si + 1) * T], in_=wsT[0:T, :, 0:T], r=['wsT'], w=['wsTs'],
                          sem='sm%d' % si)
                    t.dma('sp', out=bsTs[si * T:(si + 1) * T, :], in_=bsT[0:T, :], r=['bsT'], w=['bsTs'], sem='sn%d' % si)
            t.op('dve', lambda: nc.vector.memset(rs[:], 0.0), w=['rs_o'])

            def stageA(bi, cts=(0, 1, 2, 3)):
                b = blocks[bi]
                k = bi % 2
                hcol = b * 128
                for ct in cts:
                    bk = ct
                    t.mms([lambda kc=kc, ct=ct, bk=bk: nc.tensor.matmul(
                        self.psb(bk), lhsT=self.hT[:, kc, hcol:hcol + 128], rhs=w_in[ct][:, kc, :],
                        start=(kc == 0), stop=(kc == 7)) for kc in range(8)], r=['wsg_in%d' % ct, 'hT%d' % b], w=['ps%d' % bk])
                    dst = (zu[k] if ct < 2 else zv[k])[:, (ct % 2) * 512:(ct % 2 + 1) * 512]
                    t.op('act', lambda bk=bk, dst=dst: nc.scalar.activation(out=dst, in_=self.psb(bk), func=AF.Gelu_apprx_tanh),
                         r=['ps%d' % bk], w=['z%d_%d' % (k, ct)])

            def stageB(bi):
                b = blocks[bi]
                k = bi % 2
                samp = (b == self.SB)
                t.op('act', lambda: nc.scalar.activation(out=self.junk[:], in_=zv[k][:], func=AF.Square, accum_out=rs[:, k:k + 1]),
                     r=['z%d_2' % k, 'z%d_3' % k, 'rs_o'], w=['rs_w%d' % k])
                t.op('dve', lambda: nc.vector.tensor_scalar(out=rso[:, k:k + 1], in0=rs[:, k:k + 1], scalar1=1.0 / D, scalar2=EPS,
                                                            op0=ALU.mult, op1=ALU.add), r=['rs_w%d' % k], w=['rso%d' % k])
                t.op('pool', lambda: nc.gpsimd.tensor_tensor(out=rso[:, 2 + k:3 + k], in0=rso[:, k:k + 1], in1=self.mhalf[:], op=ALU.pow),
                     r=['rso%d' % k, 'c_mhalf'], w=['rsr%d' % k])
                t.op('dve', lambda: nc.vector.memset(rs[:, k:k + 1], 0.0), r=['rs_w%d' % k, 'rso%d' % k], w=['rs_o'])
                if samp:
                    t.op('dve', lambda: nc.vector.scalar_tensor_tensor(out=zvf[:], in0=zv[k][:], scalar=rso[:, 2 + k:3 + k], in1=gsg[:],
                                                                       op0=ALU.mult, op1=ALU.mult),
                         r=['z%d_2' % k, 'z%d_3' % k, 'rsr%d' % k, 'gsg'], w=['zvf'])
                    t.op('pool', lambda: nc.gpsimd.tensor_copy(out=zvn[k][:], in_=zvf[:]), r=['zvf'], w=['zvn%d' % k])
                    t.dma('sp', out=self.nz_s, in_=zvf[:], r=['zvf'], sem='oz')
                else:
                    t.op('dve', lambda: nc.vector.scalar_tensor_tensor(out=zvn[k][:], in0=zv[k][:], scalar=rso[:, 2 + k:3 + k], in1=gsg[:],
                                                                       op0=ALU.mult, op1=ALU.mult),
                         r=['z%d_2' % k, 'z%d_3' % k, 'rsr%d' % k, 'gsg'], w=['zvn%d' % k])

            def stageB2(bi):
                b = blocks[bi]
                k = bi % 2
                samp = (b == self.SB)
                wst = wsTs if samp else wsT
                bst = bsTs if samp else bsT
                for g in range(8):
                    mb = 4 + g // 4
                    t.mms([lambda g=g, mb=mb: nc.tensor.matmul(self.psb(mb)[:, (g % 4) * 128:(g % 4 + 1) * 128], lhsT=wst[:, g, :],
                                                            rhs=zvn[k][:, g * 128:(g + 1) * 128], start=True, stop=True)],
                          r=['wsT', 'wsTs', 'zvn%d' % k], w=['ps%d' % mb])
                    if g % 4 == 3:
                        for gg in range(g - 3, g + 1):
                            t.op('dve', lambda gg=gg, mb=mb: nc.vector.scalar_tensor_tensor(
                                out=gated[k][:, gg * 128:(gg + 1) * 128], in0=self.psb(mb)[:, (gg % 4) * 128:(gg % 4 + 1) * 128],
                                scalar=bst[:, gg:gg + 1], in1=zu[k][:, gg * 128:(gg + 1) * 128], op0=ALU.add, op1=ALU.mult),
                                r=['ps%d' % mb, 'bsT', 'bsTs', 'z%d_0' % k, 'z%d_1' % k], w=['gated%d_%d' % (k, gg)])

            def stageB3(bi):
                b = blocks[bi]
                k = bi % 2
                bank = 6
                pst = self.psb16(bank)
                t.mms([lambda c=c: nc.tensor.transpose(out=pst[:, c * 128:(c + 1) * 128], in_=gated[k][:, c * 128:(c + 1) * 128],
                                                       identity=self.identb[:]) for c in range(8)],
                      r=['gated%d_%d' % (k, g) for g in range(8)] + ['identb'], w=['ps%d' % bank])
                t.op('act', lambda: nc.scalar.copy(out=gT[k][:], in_=pst.rearrange("p (c t) -> p c t", c=8)),
                     r=['ps%d' % bank], w=['gT%d' % k])
                for dh in range(2):
                    yb = 4 + dh
                    t.mms([lambda c=c, dh=dh, yb=yb: nc.tensor.matmul(self.psb(yb), lhsT=gT[k][:, c, :], rhs=wo[dh][:, c, :],
                                                                   start=(c == 0), stop=(c == 7)) for c in range(8)],
                          r=['gT%d' % k, 'wsg_o%d' % dh], w=['ps%d' % yb])
                    xs = self.x[:, b, dh * 512:(dh + 1) * 512]
                    t.op('dve', lambda xs=xs, yb=yb: nc.vector.tensor_tensor(out=xs, in0=self.psb(yb), in1=xs, op=ALU.add),
                         r=['ps%d' % yb, 'x%d_%d' % (b, dh)], w=['x%d_%d' % (b, dh)])

            nb_ = len(blocks)
            stageA(0)
            for bi in range(nb_):
                stageB(bi)
                ctx = self.norm_pre(blocks[bi - 2], gk) if (post is not None and bi >= 2) else None
                if bi + 1 < nb_:
                    stageA(bi + 1, (0, 1))
                stageB2(bi)
                if ctx is not None:
                    self.norm_post(ctx)
                if bi + 1 < nb_:
                    stageA(bi + 1, (2, 3))
                elif post is not None and nb_ >= 2:
                    post([blocks[nb_ - 2]])
                stageB3(bi)
            if post is not None:
                post([blocks[nb_ - 1]])
            t.barrier()

    def final(self, i, hf, blocks):
        nc, t = self.nc, self.t
        with ExitStack() as es:
            yst = [self.sb(es, "yst%d" % k, [128, D]) for k in range(2)]
            t.dma('sp', out=self.gb[:], in_=self.final_g.partition_broadcast(128), w=['gb'], sem='gb')
            self.rstd_of([self.x[:, b, :] for b in blocks], [self.xk(b) for b in blocks])
            for j, b in enumerate(blocks):
                k = j % 2
                t.op('dve', lambda b=b, j=j, k=k: nc.vector.scalar_tensor_tensor(
                    out=yst[k][:], in0=self.x[:, b, :], scalar=self.rstd[:, j:j + 1], in1=self.gb[:],
                    op0=ALU.mult, op1=ALU.mult), r=self.xk(b) + ['rstd', 'gb'], w=['yst%d' % k])
                if b == self.SB:
                    dst = self.y_s
                else:
                    p0 = hf * self.PL + b * 128
                    dst = self.y_p[i, p0:p0 + 128, :]
                t.dma('sp', out=dst, in_=yst[k][:], r=['yst%d' % k], sem='oy%d' % k)
            t.barrier()


_CACHE = {}


def _get_prog(key):
    if key not in _CACHE:
        _CACHE[key] = Prog(*key)
    return _CACHE[key]


def run_cores(prog, per_core_inputs):
    res = run_bass_kernel_spmd(prog.nc, per_core_inputs, core_ids=list(range(len(per_core_inputs))))
    return res.results


def kernel(x_prompt, x_sample, cache_k, cache_v, cache_logf, state_pool, norm_g, ffn_w_in, ffn_w_down,
           even_w_in, even_b_f, pool_w, pool_scale, even_w_out, sgu_w_in, sgu_norm_g, sgu_w_s, sgu_b_s,
           sgu_w_out, final_g):
    NC = 8
    f = lambda a: np.ascontiguousarray(np.asarray(a, dtype=np.float32))
    x_prompt, x_sample = f(x_prompt), f(x_sample)
    B, S, _ = x_prompt.shape
    Bs, T, _ = x_sample.shape
    P = cache_k.shape[2]
    NP, NS = B // NC, Bs // NC
    prog = _get_prog((NP, S, 1024, NS, T, P))
    shared = {
        "norm_g": f(norm_g), "ffn_w_in": f(ffn_w_in), "ffn_w_down": f(ffn_w_down),
        "even_w_in": f(even_w_in)[0], "even_b_f": f(even_b_f)[0], "pool_w": f(pool_w)[0],
        "pool_scale": f(pool_scale)[0], "even_w_out": f(even_w_out)[0], "sgu_w_in": f(sgu_w_in)[0],
        "sgu_norm_g": f(sgu_norm_g)[0], "sgu_w_s": f(sgu_w_s)[0], "sgu_b_s": f(sgu_b_s)[0],
        "sgu_w_out": f(sgu_w_out)[0], "final_g": f(final_g),
    }
    ck, cv, clf, sp = f(cache_k)[0], f(cache_v)[0], f(cache_logf)[0], f(state_pool)[0]
    in_maps = []
    for c in range(NC):
        m = dict(shared)
        m["xp"] = x_prompt[c * NP:(c + 1) * NP]
        m["xs"] = x_sample[c * NS:(c + 1) * NS].reshape(NS * T, D)
        m["ck"] = ck[c * NS:(c + 1) * NS].reshape(NS, P, 512)
        m["cv"] = cv[c * NS:(c + 1) * NS].reshape(NS, P, 512)
        m["clf"] = clf[c * NS:(c + 1) * NS]
        m["spool"] = sp[c * NS:(c + 1) * NS]
        in_maps.append(m)
    res = run_cores(prog, in_maps)
    cat = lambda k: np.concatenate([r[k] for r in res], axis=0)
    y_p = cat("y_p")
    y_s = cat("y_s").reshape(Bs, T, D)
    nk_p = cat("nk_p").reshape(1, B, S, H, DH)
    nv_p = cat("nv_p").reshape(1, B, S, H, DH)
    nlf_p = cat("nlf_p").reshape(1, B, S, H)
    npool_p = cat("npool_p").reshape(1, B, 15, 512)
    nk_s = cat("nk_s").reshape(1, Bs, T, H, DH)
    nv_s = cat("nv_s").reshape(1, Bs, T, H, DH)
    nlf_s = cat("nlf_s").reshape(1, Bs, T, H)
    npool_s = cat("npool_s").reshape(1, Bs, 15, 512)
    nz_s = cat("nz_s").reshape(1, Bs, T, D)
    return (y_p, y_s, nk_p, nv_p, nlf_p, npool_p, nk_s, nv_s, nlf_s, npool_s, nz_s)
```

```python
import numpy as np
from contextlib import ExitStack
import concourse.bass as bass
import concourse.mybir as mybir
from concourse.bass_utils import run_bass_kernel_spmd

F32 = mybir.dt.float32
BF16 = mybir.dt.bfloat16
AF = mybir.ActivationFunctionType
ALU = mybir.AluOpType

D = 1024
DFF = 2816
NGRP = 11
H = 8
DH = 64
EVEN_IN = 2056
EPS = 1e-6
SAME_ENGINE_SYNC = True


class Trk:
    def __init__(self, nc, es):
        self.nc = nc
        self.es = es
        self.E = {'pe': nc.tensor, 'act': nc.scalar, 'dve': nc.vector, 'pool': nc.gpsimd, 'sp': nc.sync}
        self.sem = {}
        self.cnt = {}
        self.waited = {e: {} for e in self.E}
        for e in self.E:
            self.sem[e] = es.enter_context(nc.semaphore("s_" + e))
            self.cnt[e] = 0
        self.lastw = {}
        self.readers = {}
        self.dsem = {}
        self.nwait = 0
        self.dead = False

    def _wait(self, e, r, w):
        need = {}
        for k in list(r) + list(w):
            if k in self.lastw:
                s, v = self.lastw[k]
                if s == e and (e == 'pe' or not SAME_ENGINE_SYNC):
                    continue
                need[s] = max(need.get(s, 0), v)
        for k in w:
            for (s, v) in self.readers.get(k, ()):
                if s == e:
                    continue
                need[s] = max(need.get(s, 0), v)
        for s, v in need.items():
            if self.waited[e].get(s, 0) >= v:
                continue
            if s in self.E:
                h = self.sem[s]
            else:
                d = self.dsem[s]
                assert d[1] == v, "dma sem %s waited at %d but issued %d" % (s, v, d[1])
                h = d[0]
            self.E[e].wait_ge(h, v)
            self.waited[e][s] = v
            self.nwait += 1

    def _mark(self, tok, r, w):
        for k in r:
            self.readers.setdefault(k, []).append(tok)
        for k in w:
            self.lastw[k] = tok
            self.readers[k] = []

    def op(self, e, fn, r=(), w=()):
        if self.dead:
            return
        self._wait(e, r, w)
        inst = fn()
        inst.then_inc(self.sem[e], 1)
        self.cnt[e] += 1
        self._mark((e, self.cnt[e]), r, w)

    def mms(self, fns, r=(), w=()):
        e = 'pe'
        if self.dead:
            return
        self._wait(e, r, w)
        inst = None
        for fn in fns:
            inst = fn()
        inst.then_inc(self.sem[e], 1)
        self.cnt[e] += 1
        self._mark((e, self.cnt[e]), r, w)

    def dma(self, q, out, in_, r=(), w=(), sem=None):
        if self.dead:
            return
        self._wait(q, r, w)
        if sem not in self.dsem:
            self.dsem[sem] = [self.es.enter_context(self.nc.semaphore("d_" + sem)), 0]
        d = self.dsem[sem]
        d[1] += 16
        self.E[q].dma_start(out=out, in_=in_).then_inc(d[0], 16)
        self._mark((sem, d[1]), r, w)

    def barrier(self):
        if self.dead:
            return
        for e in self.E:
            for s in self.E:
                if s != e and self.cnt[s] > self.waited[e].get(s, 0):
                    self.E[e].wait_ge(self.sem[s], self.cnt[s])
                    self.waited[e][s] = self.cnt[s]
            for s, d in self.dsem.items():
                if d[1] > self.waited[e].get(s, 0):
                    self.E[e].wait_ge(d[0], d[1])
                    self.waited[e][s] = d[1]
        self.lastw = {}
        self.readers = {}

    def finish(self):
        for s, d in self.dsem.items():
            if d[1] > self.waited['sp'].get(s, 0):
                self.nc.sync.wait_ge(d[0], d[1])
        for s in self.E:
            if s != 'sp' and self.cnt[s] > self.waited['sp'].get(s, 0):
                self.nc.sync.wait_ge(self.sem[s], self.cnt[s])


def bc(ap, shape):
    return ap.to_broadcast(list(shape))


class StopBuild(Exception):
    pass


class Prog:
    def chk(self, name):
        if self.stop_at == name and not self.t.dead:
            self.t.barrier()
            self.t.dead = True

    def __init__(self, NP, S, PL, NS, T, P, stop_at=None):
        self.stop_at = stop_at
        self.NP, self.S, self.PL, self.NS, self.T, self.P = NP, S, PL, NS, T, P
        assert NS * T == 128 and S % PL == 0 and PL % 128 == 0 and P % 128 == 0
        self.NBP = PL // 128
        self.NH = S // PL
        self.NBS = S // 128
        self.NBC = P // 128
        self.TW = min(512, PL)
        self.SB = self.NBP
        self.HC = PL + 128
        self.nc = bass.Bass("TRN2", target_bir_lowering=False)
        self.yq = 0
        self.wq = 0
        self.tq = 0
        self.build()

    def dram(self):
        nc = self.nc
        NP, S, NS, T, P = self.NP, self.S, self.NS, self.T, self.P

        def I(name, shape):
            return nc.dram_tensor(name, list(shape), F32, kind="ExternalInput").ap()

        def O(name, shape):
            return nc.dram_tensor(name, list(shape), F32, kind="ExternalOutput").ap()

        self.xp = I("xp", [NP, S, D])
        self.xs = I("xs", [NS * T, D])
        self.ck = I("ck", [NS, P, 512])
        self.cv = I("cv", [NS, P, 512])
        self.clf = I("clf", [NS, P, H])
        self.spool = I("spool", [NS, 15, 512])
        self.norm_g = I("norm_g", [2, 3, D])
        self.ffn_w_in = I("ffn_w_in", [2, 2, D, 2 * DFF])
        self.ffn_w_down = I("ffn_w_down", [2, 2, DFF, D])
        self.even_w_in = I("even_w_in", [D, EVEN_IN])
        self.even_b_f = I("even_b_f", [H])
        self.pool_w = I("pool_w", [4, 128, 128])
        self.pool_scale = I("pool_scale", [512])
        self.even_w_out = I("even_w_out", [D, D])
        self.sgu_w_in = I("sgu_w_in", [D, 2 * D])
        self.sgu_norm_g = I("sgu_norm_g", [D])
        self.sgu_w_s = I("sgu_w_s", [8, 128, 128])
        self.sgu_b_s = I("sgu_b_s", [8, 128])
        self.sgu_w_out = I("sgu_w_out", [D, D])
        self.final_g = I("final_g", [D])
        self.y_p = O("y_p", [NP, S, D])
        self.y_s = O("y_s", [NS * T, D])
        self.nk_p = O("nk_p", [NP, S, 512])
        self.nv_p = O("nv_p", [NP, S, 512])
        self.nlf_p = O("nlf_p", [NP, S, H])
        self.npool_p = O("npool_p", [NP, 15, 512])
        self.nk_s = O("nk_s", [NS, T, 512])
        self.nv_s = O("nv_s", [NS, T, 512])
        self.nlf_s = O("nlf_s", [NS, T, H])
        self.npool_s = O("npool_s", [NS, 15, 512])
        self.nz_s = O("nz_s", [NS * T, D])

    def sb(self, es, name, shape, dt=F32):
        self.uid = getattr(self, 'uid', 0) + 1
        return es.enter_context(self.nc.sbuf_tensor("%s_%d" % (name, self.uid), list(shape), dt))

    def psb(self, i):
        return self.ps[:, i * 512:(i + 1) * 512]

    def psb16(self, i):
        return self.ps[:, i * 512:(i + 1) * 512].bitcast(BF16)

    def build(self):
        nc = self.nc
        self.dram()
        with ExitStack() as es:
            self.t = t = Trk(nc, es)
            sb = lambda n, sh, dt=F32: self.sb(es, n, sh, dt)
            self.ps = es.enter_context(nc.psum_tensor("ps", [128, 4096], F32))
            self.x = sb("x", [128, self.NBP + 1, D])
            self.hT = sb("hT", [128, 8, self.HC], BF16)
            self.KT = sb("KT", [128, 4, self.S], BF16)
            self.Vaug = sb("Vaug", [128, self.NBS, H, 65], BF16)
            self.ctm = sb("ctm", [128, self.NBS, H])
            self.carry = sb("carry", [128, H])
            self.hist = sb("hist", [128, 4, 16])
            self.gb = sb("gb", [128, D])
            self.ss = sb("ss", [128, 16])
            self.rt = sb("rt", [128, 16])
            self.rstd = sb("rstd", [128, 16])
            self.junk = sb("junk", [128, D], BF16)
            self.gbuf = [sb("gbuf%d" % k, [128, D]) for k in range(2)]
            self.hbn = [sb("hbn%d" % k, [128, D], BF16) for k in range(2)]
            self.nst = sb("nst", [128, 8])
            self.nst2 = sb("nst2", [128, 3, 8])
            self.gq = 0
            self.nq = 0
            self.norm2banks = False
            self.fq = 0
            self.nq2 = 0
            self.hq = 0
            self.consts(es)
            npass = 0
            try:
                self.chk("consts")
                for i in range(self.NP):
                    for hf in range(self.NH):
                        self.do_pass(i, hf, ride=(npass == 0))
                        npass += 1
            except StopBuild:
                pass
            t.dead = False
            t.barrier()
            t.finish()

    def consts(self, es):
        nc, t = self.nc, self.t
        sb = lambda n, sh, dt=F32: self.sb(es, n, sh, dt)
        self.allones = sb("allones", [128, 128])
        self.identf = sb("identf", [128, 128])
        self.identb = sb("identb", [128, 128], BF16)
        self.triinc = sb("triinc", [128, 128])
        self.trimask = sb("trimask", [128, 128], BF16)
        self.sel64 = sb("sel64", [128, 128])
        self.sel16 = sb("sel16", [128, 128])
        self.sel0 = sb("sel0", [128, 128])
        self.eps_t = sb("eps_t", [128, 1])
        self.one_t = sb("one_t", [128, 1])
        self.invc = sb("invc", [128, 16])
        self.mhalf = sb("mhalf", [128, 1])
        P = lambda fn, r=(), w=(): t.op('pool', fn, r, w)
        P(lambda: nc.gpsimd.memset(self.allones[:], 1.0), w=['allones'])
        P(lambda: nc.gpsimd.memset(self.eps_t[:], EPS), w=['c_eps'])
        P(lambda: nc.gpsimd.memset(self.one_t[:], 1.0), w=['c_one'])
        P(lambda: nc.gpsimd.memset(self.mhalf[:], -0.5), w=['c_mhalf'])
        for j in range(16):
            P(lambda j=j: nc.gpsimd.memset(self.invc[:, j:j + 1], 1.0 / (j + 1)), w=['c_invc'])
        P(lambda: nc.gpsimd.affine_select(out=self.identf[:], in_=self.allones[:], pattern=[[1, 128]],
                                          compare_op=ALU.is_equal, fill=0.0, base=0, channel_multiplier=-1),
          r=['allones'], w=['identf'])
        P(lambda: nc.gpsimd.affine_select(out=self.triinc[:], in_=self.allones[:], pattern=[[1, 128]],
                                          compare_op=ALU.is_ge, fill=0.0, base=0, channel_multiplier=-1),
          r=['allones'], w=['triinc'])
        P(lambda: nc.gpsimd.affine_select(out=self.sel64[:], in_=self.allones[:], pattern=[[0, 128]],
                                          compare_op=ALU.is_equal, fill=0.0, base=-64, channel_multiplier=1),
          r=['allones'], w=['sel64'])
        P(lambda: nc.gpsimd.affine_select(out=self.sel16[:], in_=self.allones[:], pattern=[[0, 128]],
                                          compare_op=ALU.is_equal, fill=0.0, base=-16, channel_multiplier=1),
          r=['allones'], w=['sel16'])
        P(lambda: nc.gpsimd.affine_select(out=self.sel0[:], in_=self.allones[:], pattern=[[0, 128]],
                                          compare_op=ALU.is_equal, fill=0.0, base=0, channel_multiplier=1),
          r=['allones'], w=['sel0'])
        P(lambda: nc.gpsimd.tensor_copy(out=self.identb[:], in_=self.identf[:]), r=['identf'], w=['identb'])
        P(lambda: nc.gpsimd.tensor_copy(out=self.trimask[:], in_=self.triinc[:]), r=['triinc'], w=['trimask'])
        P(lambda: nc.gpsimd.memset(self.nst[:], 0.0), w=['nst0', 'nst1'])
        P(lambda: nc.gpsimd.memset(self.nst2[:], 0.0), w=['nz0', 'nz1'])
        P(lambda: nc.gpsimd.memset(self.Vaug[:, :, :, 64:65], 1.0), w=['vones'])
        t.barrier()

    def xk(self, b):
        return ['x%d_0' % b, 'x%d_1' % b]

    def do_pass(self, i, hf, ride):
        nc, t = self.nc, self.t
        NBP, PL = self.NBP, self.PL
        blocks = list(range(NBP)) + ([self.SB] if ride else [])
        pos0 = hf * PL
        pre = getattr(self, 'x_pre', set())
        for b in range(NBP):
            if b in pre:
                continue
            t.dma('sp', out=self.x[:, b, :], in_=self.xp[i, pos0 + b * 128:pos0 + (b + 1) * 128, :], w=self.xk(b), sem='xin_%d' % b)
        self.x_pre = set()
        nxt_pass = (i, hf + 1) if hf + 1 < self.NH else ((i + 1, 0) if i + 1 < self.NP else None)
        self.next_x = nxt_pass
        if ride:
            t.dma('sp', out=self.x[:, self.SB, :], in_=self.xs, w=self.xk(self.SB), sem='xin2')
        tiles = []
        for c0 in range(0, PL, self.TW):
            tiles.append((c0, self.TW, list(range(c0 // 128, (c0 + self.TW) // 128))))
        if ride:
            tiles.append((PL, 128, [self.SB]))
        self.chk("xload")
        if getattr(self, 'pre_normed', False):
            self.pre_normed = False
        else:
            self.norm_to_hT(self.norm_g[0, 0], blocks)
        last_pass = (i == self.NP - 1 and hf == self.NH - 1)
        for l in range(2):
            gk = self.load_gain(self.norm_g[l, 1])
            self.ffn_open()
            self.ffn(l, 0, blocks, tiles, post=lambda bl, gk=gk: self.norm_blocks(bl, gk), nxt=None)
            self.ffn_close()
            self.chk("ffn%d0" % l)
            gk = self.load_gain(self.norm_g[l, 2])
            post = lambda bl, gk=gk: self.norm_blocks(bl, gk)
            if l == 0:
                self.even_prompt(i, hf, post)
                self.chk("evenp")
                if ride:
                    self.even_sample(post)
                    self.chk("evens")
            else:
                self.odd(i, hf, blocks, post, gk)
                self.chk("odd")
            self.ffn_open()
            if l == 0:
                gk = self.load_gain(self.norm_g[1, 0])
                self.ffn(l, 1, blocks, tiles, post=lambda bl, gk=gk: self.norm_blocks(bl, gk), nxt=(1, 0))
            else:
                gk = self.load_gain(self.final_g)
                gk_first = None if last_pass else self.load_gain(self.norm_g[0, 0])
                self.ffn(l, 1, blocks, tiles, final=(gk, i, hf, gk_first), nxt=(None if last_pass else (0, 0)))
                if gk_first is not None:
                    self.pre_normed = True
        if last_pass:
            self.ffn_close()
        self.chk("pass")

    def ffn_open(self):
        if getattr(self, 'fes', None) is not None:
            return
        self.fes = es = ExitStack()
        es.__enter__()
        fb = {}
        fb['stin'] = [self.sb(es, "stin%d" % k, [128, 8, 2, 256]) for k in range(2)]
        fb['stdn'] = [self.sb(es, "stdn%d" % k, [128, 2, D]) for k in range(2)]
        fb['wbi'] = [self.sb(es, "wbi%d" % k, [128, 8, 2, 256], BF16) for k in range(2)]
        fb['wbd'] = [self.sb(es, "wbd%d" % k, [128, 2, D], BF16) for k in range(2)]
        fb['sg'] = [self.sb(es, "sg%d" % k, [128, 512]) for k in range(2)]
        fb['act'] = [[self.sb(es, "act%d_%d" % (p, m), [128, 512], BF16) for m in range(2)] for p in range(2)]
        fb['yst'] = [self.sb(es, "yst%d" % k, [128, D]) for k in range(2)]
        self.fb = fb
        self.preloaded = 0
        self.precast = 0

    def ffn_close(self):
        self.t.barrier()
        self.fes.__exit__(None, None, None)
        self.fes = None

    def rstd_of(self, srcs, keys_list, n_feat=D):
        nc, t = self.nc, self.t
        n = len(srcs)
        t.op('dve', lambda: nc.vector.memset(self.ss[:, 0:n], 0.0), w=['ss'])
        for j, (a, ks) in enumerate(zip(srcs, keys_list)):
            t.op('act', lambda a=a, j=j: nc.scalar.activation(out=self.junk[:, 0:n_feat], in_=a, func=AF.Square,
                                                             accum_out=self.ss[:, j:j + 1]),
                 r=list(ks) + ['ss'], w=['ssw%d' % j])
        t.op('act', lambda: nc.scalar.activation(out=self.rt[:, 0:n], in_=self.ss[:, 0:n], func=AF.Sqrt,
                                                 bias=self.eps_t[:], scale=1.0 / n_feat),
             r=['ssw%d' % j for j in range(n)] + ['c_eps', 'ss'], w=['rt'])
        t.op('dve', lambda: nc.vector.reciprocal(out=self.rstd[:, 0:n], in_=self.rt[:, 0:n]), r=['rt'], w=['rstd'])

    def load_gain(self, gain_ap):
        k = self.gq % 2
        self.gq += 1
        self.t.dma('sp', out=self.gbuf[k][:], in_=gain_ap.partition_broadcast(128), w=['gbuf%d' % k], sem='gbf%d' % k)
        return k

    def rstd_block(self, b):
        nc, t = self.nc, self.t
        k = self.nq % 2
        self.nq += 1
        nst = self.nst
        t.op('act', lambda: nc.scalar.activation(out=self.junk[:], in_=self.x[:, b, :], func=AF.Square, accum_out=nst[:, k:k + 1]),
             r=self.xk(b) + ['nst%d' % k], w=['nsq%d' % k])
        t.op('dve', lambda: nc.vector.tensor_scalar(out=nst[:, 2 + k:3 + k], in0=nst[:, k:k + 1], scalar1=1.0 / D, scalar2=EPS,
                                                    op0=ALU.mult, op1=ALU.add), r=['nsq%d' % k], w=['nrt%d' % k])
        t.op('pool', lambda: nc.gpsimd.tensor_tensor(out=nst[:, 4 + k:5 + k], in0=nst[:, 2 + k:3 + k], in1=self.mhalf[:], op=ALU.pow),
             r=['nrt%d' % k, 'c_mhalf'], w=['nrs%d' % k])
        t.op('dve', lambda: nc.vector.memset(nst[:, k:k + 1], 0.0), r=['nsq%d' % k, 'nrt%d' % k], w=['nst%d' % k])
        return k

    def _rstd_list(self, bl):
        nc, t = self.nc, self.t
        n = len(bl)
        assert n <= 4
        par = self.nq2 % 2
        self.nq2 += 1
        base = par * 4
        st = self.nst2
        for j, b in enumerate(bl):
            t.op('act', lambda j=j, b=b: nc.scalar.activation(out=self.junk[:], in_=self.x[:, b, :], func=AF.Square,
                                                             accum_out=st[:, 0, base + j:base + j + 1]),
                 r=self.xk(b) + ['nz%d' % par], w=['nsq%d_%d' % (par, j)])
        sq = ['nsq%d_%d' % (par, j) for j in range(n)]
        t.op('dve', lambda: nc.vector.tensor_scalar(out=st[:, 1, base:base + n], in0=st[:, 0, base:base + n], scalar1=1.0 / D,
                                                    scalar2=EPS, op0=ALU.mult, op1=ALU.add), r=sq, w=['nrt2_%d' % par])
        t.op('pool', lambda: nc.gpsimd.tensor_tensor(out=st[:, 2, base:base + n], in0=st[:, 1, base:base + n],
                                                     in1=bc(self.mhalf[:], [128, n]), op=ALU.pow),
             r=['nrt2_%d' % par, 'c_mhalf'], w=['nrs2_%d' % par])
        t.op('dve', lambda: nc.vector.memset(st[:, 0, base:base + n], 0.0), r=sq + ['nrt2_%d' % par], w=['nz%d' % par])
        return st, base, par

    def norm_blocks(self, bl, gk):
        nc, t = self.nc, self.t
        st, base, par = self._rstd_list(bl)
        for j, b in enumerate(bl):
            k = self.hq % 2
            self.hq += 1
            t.op('dve', lambda j=j, b=b, k=k: nc.vector.scalar_tensor_tensor(
                out=self.hbn[k][:], in0=self.x[:, b, :], scalar=st[:, 2, base + j:base + j + 1], in1=self.gbuf[gk][:],
                op0=ALU.mult, op1=ALU.mult), r=self.xk(b) + ['nrs2_%d' % par, 'gbuf%d' % gk], w=['hbn%d' % k])
            nbk = 7 - (k if self.norm2banks else 0)
            pst = self.psb16(nbk)
            t.mms([lambda c=c, k=k, pst=pst: nc.tensor.transpose(out=pst[:, c * 128:(c + 1) * 128], in_=self.hbn[k][:, c * 128:(c + 1) * 128],
                                                                 identity=self.identb[:]) for c in range(8)], r=['hbn%d' % k, 'identb'], w=['ps%d' % nbk])
            t.op('act', lambda b=b, pst=pst: nc.scalar.copy(out=self.hT[:, :, b * 128:(b + 1) * 128], in_=pst.rearrange("p (c t) -> p c t", c=8)),
                 r=['ps%d' % nbk], w=['hT%d' % b])

    def norm_pre(self, b, gk):
        nc, t = self.nc, self.t
        par = self.nq2 % 2
        self.nq2 += 1
        base = par * 4
        st = self.nst2
        t.op('act', lambda: nc.scalar.activation(out=self.junk[:], in_=self.x[:, b, :], func=AF.Square,
                                                 accum_out=st[:, 0, base:base + 1]),
             r=self.xk(b) + ['nz%d' % par], w=['nsq%d_0' % par])
        t.op('dve', lambda: nc.vector.tensor_scalar(out=st[:, 1, base:base + 1], in0=st[:, 0, base:base + 1], scalar1=1.0 / D,
                                                    scalar2=EPS, op0=ALU.mult, op1=ALU.add), r=['nsq%d_0' % par], w=['nrt2_%d' % par])
        t.op('pool', lambda: nc.gpsimd.tensor_tensor(out=st[:, 2, base:base + 1], in0=st[:, 1, base:base + 1],
                                                     in1=self.mhalf[:], op=ALU.pow), r=['nrt2_%d' % par, 'c_mhalf'], w=['nrs2_%d' % par])
        t.op('dve', lambda: nc.vector.memset(st[:, 0, base:base + 1], 0.0), r=['nsq%d_0' % par, 'nrt2_%d' % par], w=['nz%d' % par])
        k = self.hq % 2
        self.hq += 1
        t.op('dve', lambda: nc.vector.scalar_tensor_tensor(
            out=self.hbn[k][:], in0=self.x[:, b, :], scalar=st[:, 2, base:base + 1], in1=self.gbuf[gk][:],
            op0=ALU.mult, op1=ALU.mult), r=self.xk(b) + ['nrs2_%d' % par, 'gbuf%d' % gk], w=['hbn%d' % k])
        return (b, k)

    def norm_post(self, ctx):
        nc, t = self.nc, self.t
        b, k = ctx
        pst = self.psb16(7)
        t.mms([lambda c=c: nc.tensor.transpose(out=pst[:, c * 128:(c + 1) * 128], in_=self.hbn[k][:, c * 128:(c + 1) * 128],
                                               identity=self.identb[:]) for c in range(8)], r=['hbn%d' % k, 'identb'], w=['ps7'])
        t.op('act', lambda: nc.scalar.copy(out=self.hT[:, :, b * 128:(b + 1) * 128], in_=pst.rearrange("p (c t) -> p c t", c=8)),
             r=['ps7'], w=['hT%d' % b])

    def final_blocks(self, bl, gk, yst, i, hf):
        nc, t = self.nc, self.t
        st, base, par = self._rstd_list(bl)
        for j, b in enumerate(bl):
            k = self.fq % 2
            self.fq += 1
            t.op('dve', lambda j=j, b=b, k=k: nc.vector.scalar_tensor_tensor(
                out=yst[k][:], in0=self.x[:, b, :], scalar=st[:, 2, base + j:base + j + 1], in1=self.gbuf[gk][:],
                op0=ALU.mult, op1=ALU.mult), r=self.xk(b) + ['nrs2_%d' % par, 'gbuf%d' % gk], w=['yst%d' % k])
            if b == self.SB:
                dst = self.y_s
            else:
                p0 = hf * self.PL + b * 128
                dst = self.y_p[i, p0:p0 + 128, :]
            t.dma('sp', out=dst, in_=yst[k][:], r=['yst%d' % k], sem='oy%d' % k)
            nx = getattr(self, 'next_x', None)
            if nx is not None and b != self.SB:
                i2, hf2 = nx
                p2 = hf2 * self.PL + b * 128
                t.dma('sp', out=self.x[:, b, :], in_=self.xp[i2, p2:p2 + 128, :], w=self.xk(b), sem='xin_%d' % b)
                self.x_pre.add(b)

    def final_block(self, b, gk, yst, i, hf):
        nc, t = self.nc, self.t
        k = self.rstd_block(b)
        nst = self.nst
        t.op('dve', lambda: nc.vector.scalar_tensor_tensor(out=yst[k][:], in0=self.x[:, b, :], scalar=nst[:, 4 + k:5 + k],
                                                           in1=self.gbuf[gk][:], op0=ALU.mult, op1=ALU.mult),
             r=self.xk(b) + ['nrs%d' % k, 'gbuf%d' % gk], w=['yst%d' % k])
        if b == self.SB:
            dst = self.y_s
        else:
            p0 = hf * self.PL + b * 128
            dst = self.y_p[i, p0:p0 + 128, :]
        t.dma('sp', out=dst, in_=yst[k][:], r=['yst%d' % k], sem='oy%d' % k)
        nx = getattr(self, 'next_x', None)
        if nx is not None and b != self.SB:
            i2, hf2 = nx
            p2 = hf2 * self.PL + b * 128
            t.dma('sp', out=self.x[:, b, :], in_=self.xp[i2, p2:p2 + 128, :], w=self.xk(b), sem='xin_%d' % b)
            self.x_pre.add(b)

    def norm_to_hT(self, gain_ap, blocks):
        nc, t = self.nc, self.t
        with ExitStack() as es:
            hb = [self.sb(es, "hb%d" % k, [128, D], BF16) for k in range(2)]
            t.dma('sp', out=self.gb[:], in_=gain_ap.partition_broadcast(128), w=['gb'], sem='gb')
            self.rstd_of([self.x[:, b, :] for b in blocks], [self.xk(b) for b in blocks])
            for j, b in enumerate(blocks):
                k = j % 2
                t.op('dve', lambda b=b, j=j, k=k: nc.vector.scalar_tensor_tensor(
                    out=hb[k][:], in0=self.x[:, b, :], scalar=self.rstd[:, j:j + 1], in1=self.gb[:],
                    op0=ALU.mult, op1=ALU.mult), r=self.xk(b) + ['rstd', 'gb'], w=['hb%d' % k])
                bank = 6 + k
                pst = self.psb16(bank)
                t.mms([lambda c=c, k=k, pst=pst: nc.tensor.transpose(out=pst[:, c * 128:(c + 1) * 128],
                                                                  in_=hb[k][:, c * 128:(c + 1) * 128],
                                                                  identity=self.identb[:]) for c in range(8)],
                      r=['hb%d' % k, 'identb'], w=['ps%d' % bank])
                t.op('act', lambda b=b, pst=pst: nc.scalar.copy(out=self.hT[:, :, b * 128:(b + 1) * 128],
                                                              in_=pst.rearrange("p (c t) -> p c t", c=8)),
                     r=['ps%d' % bank], w=['hT%d' % b])
            t.barrier()

    def ffn(self, l, f, blocks, tiles, post=None, final=None, nxt=None):
        nc, t = self.nc, self.t
        fb = self.fb
        stin, stdn, wbi, wbd, sg, act = fb['stin'], fb['stdn'], fb['wbi'], fb['wbd'], fb['sg'], fb['act']
        self.norm2banks = True
        if final is not None:
            yst = fb['yst']

            pend_norm = []

            def post(bl):
                self.final_blocks(bl, final[0], yst, final[1], final[2])
                pb = [b for b in bl if b != self.SB]
                if final[3] is not None and pb:
                    pend_norm.append(pb)
        g0 = self.wq
        self.wq += NGRP

        def load(lf, g, gg):
            k = gg % 2
            win = self.ffn_w_in[lf[0], lf[1]].rearrange("(kc p) (two ff) -> p kc two ff", p=128, two=2)
            wdn = self.ffn_w_down[lf[0], lf[1]]
            for gu in range(2):
                t.dma('sp', out=stin[k][:, :, gu, :], in_=win[:, :, gu, g * 256:(g + 1) * 256], w=['stin%d' % k], sem='sti%d' % k)
            t.dma('sp', out=stdn[k][:], in_=wdn[g * 256:(g + 1) * 256, :].rearrange("(m p) d -> p m d", p=128),
                  w=['stdn%d' % k], sem='std%d' % k)

        def cast(gg):
            k = gg % 2
            t.op('act', lambda: nc.scalar.copy(out=wbi[k][:], in_=stin[k][:]), r=['stin%d' % k], w=['wbi%d' % k])
            t.op('act', lambda: nc.scalar.copy(out=wbd[k][:], in_=stdn[k][:]), r=['stdn%d' % k], w=['wbd%d' % k])

        def inproj(gg, ti, par):
            k = gg % 2
            c0, w, tb = tiles[ti]
            for m in range(2):
                for gu in range(2):
                    bank = m * 2 + gu
                    t.mms([lambda kc=kc, gu=gu, m=m, bank=bank: nc.tensor.matmul(
                        self.psb(bank)[:, 0:w], lhsT=wbi[k][:, kc, gu, m * 128:(m + 1) * 128],
                        rhs=self.hT[:, kc, c0:c0 + w], start=(kc == 0), stop=(kc == 7)) for kc in range(8)],
                        r=['wbi%d' % k] + ['hT%d' % b for b in tb], w=['ps%d' % bank])
                t.op('act', lambda m=m: nc.scalar.activation(out=sg[m][:, 0:w], in_=self.psb(m * 2)[:, 0:w], func=AF.Silu),
                     r=['ps%d' % (m * 2)], w=['sg%d' % m])
                t.op('dve', lambda m=m: nc.vector.tensor_tensor(out=act[par][m][:, 0:w], in0=sg[m][:, 0:w],
                                                               in1=self.psb(m * 2 + 1)[:, 0:w], op=ALU.mult),
                     r=['sg%d' % m, 'ps%d' % (m * 2 + 1)], w=['act%d_%d' % (par, m)])

        pend = []

        def down(g, gg, ti, par):
            k = gg % 2
            c0, w, tb = tiles[ti]
            for bi, b in enumerate(tb):
                for dh in range(2):
                    yb = 4 + (self.yq % 4)
                    self.yq += 1
                    t.mms([lambda m=m, bi=bi, dh=dh, yb=yb: nc.tensor.matmul(
                        self.psb(yb), lhsT=act[par][m][:, bi * 128:(bi + 1) * 128],
                        rhs=wbd[k][:, m, dh * 512:(dh + 1) * 512], start=(m == 0), stop=(m == 1)) for m in range(2)],
                        r=['act%d_%d' % (par, 0), 'act%d_%d' % (par, 1), 'wbd%d' % k], w=['ps%d' % yb])
                    xs = self.x[:, b, dh * 512:(dh + 1) * 512]
                    t.op('dve', lambda xs=xs, yb=yb: nc.vector.scalar_tensor_tensor(
                        out=xs, in0=self.psb(yb), scalar=0.5, in1=xs, op0=ALU.mult, op1=ALU.add),
                        r=['ps%d' % yb, 'x%d_%d' % (b, dh)], w=['x%d_%d' % (b, dh)])
            if g == NGRP - 1 and post is not None:
                if pend:
                    post(list(pend))
                pend[:] = list(tb)

        for g in range(self.preloaded, 2):
            load((l, f), g, g0 + g)
        precast = self.precast
        self.preloaded = 0
        self.precast = 0
        units = [(g, ti) for g in range(NGRP) for ti in range(len(tiles))]
        prev = None
        for u, (g, ti) in enumerate(units):
            if ti == 0:
                if g >= precast:
                    cast(g0 + g)
                if g + 2 < NGRP:
                    load((l, f), g + 2, g0 + g + 2)
                elif nxt is not None:
                    load(nxt, g + 2 - NGRP, g0 + g + 2)
                    self.preloaded += 1
            inproj(g0 + g, ti, u % 2)
            if prev is not None:
                down(prev[0], g0 + prev[0], prev[1], (u - 1) % 2)
                if nxt is not None and prev == (NGRP - 2, len(tiles) - 1):
                    cast(g0 + NGRP)
                    self.precast = 1
            prev = (g, ti)
        down(prev[0], g0 + prev[0], prev[1], (len(units) - 1) % 2)
        if post is not None and pend:
            post(list(pend))
        if final is not None:
            for pb in pend_norm:
                self.norm_blocks(pb, final[3])
        if nxt is not None:
            cast(g0 + NGRP + 1)
            self.precast = 2
        self.norm2banks = False

    def load_w_bf16(self, es, name, src_ap, ncols, stage, wt=None):
        nc, t = self.nc, self.t
        if wt is None:
            wt = self.sb(es, name, [128, 8, ncols], BF16)
        if not hasattr(self, 'scr'):
            self.scr = {}
            self.castq = 0
        if name not in self.scr:
            scr = nc.dram_tensor("scr_" + name, [128, 8 * ncols], BF16, kind="Internal").ap()
            self.scr[name] = scr
            t.dma('sp', out=stage[:, :, 0:ncols], in_=src_ap.rearrange("(kc p) c -> p kc c", p=128), w=['wstage'], sem='wst')
            self.castq += 1
            if self.castq % 2 == 0:
                t.op('act', lambda: nc.scalar.copy(out=wt[:], in_=stage[:, :, 0:ncols]), r=['wstage'], w=[name])
            else:
                t.op('dve', lambda: nc.vector.tensor_copy(out=wt[:], in_=stage[:, :, 0:ncols]), r=['wstage'], w=[name])
            t.dma('sp', out=scr.rearrange("p (kc c) -> p kc c", kc=8), in_=wt[:], r=[name], w=['scr_' + name], sem='scw_' + name)
        else:
            t.dma('sp', out=wt[:], in_=self.scr[name].rearrange("p (kc c) -> p kc c", kc=8), r=['scr_' + name], w=[name],
                  sem='wl_' + name)
        return wt

    def cumsum_blocks(self, lf, nb, cdst, carry_in, update_carry=True):
        nc, t = self.nc, self.t
        n = nb * H
        t.mms([lambda: nc.tensor.matmul(self.psb(0)[:, 0:n], lhsT=self.triinc[:], rhs=lf.rearrange("p b h -> p (b h)"),
                                        start=True, stop=True)], r=['lf', 'triinc'], w=['ps0'])
        t.mms([lambda: nc.tensor.matmul(self.psb(1)[:, 0:n], lhsT=self.allones[:], rhs=lf.rearrange("p b h -> p (b h)"),
                                        start=True, stop=True)], r=['lf', 'allones'], w=['ps1'])
        off = self.offt
        if carry_in is None:
            t.op('dve', lambda: nc.vector.memset(off[:, 0, :], 0.0), w=['off'])
        else:
            t.op('dve', lambda: nc.vector.tensor_copy(out=off[:, 0, :], in_=carry_in), r=['carry'], w=['off'])
        for b in range(1, nb + 1):
            t.op('dve', lambda b=b: nc.vector.tensor_tensor(out=off[:, b, :], in0=self.psb(1)[:, (b - 1) * H:b * H],
                                                         in1=off[:, b - 1, :], op=ALU.add), r=['ps1', 'off'], w=['off'])
        t.op('dve', lambda: nc.vector.tensor_tensor(out=cdst, in0=self.psb(0)[:, 0:n].rearrange("p (b h) -> p b h", h=H),
                                                    in1=off[:, 0:nb, :], op=ALU.add), r=['ps0', 'off'], w=['ctm'])
        if update_carry:
            t.op('dve', lambda: nc.vector.tensor_copy(out=self.carry[:], in_=off[:, nb, :]), r=['off'], w=['carry'])

    def logsig(self, psrc, nb, bfb, lf):
        nc, t = self.nc, self.t
        n = nb * H
        t.op('dve', lambda: nc.vector.tensor_tensor(out=lf, in0=psrc.rearrange("p (b h) -> p b h", h=H),
                                                    in1=bc(bfb[:].unsqueeze(1), [128, nb, H]), op=ALU.add),
             r=['ps5', 'bfb'], w=['lf'])
        t.op('act', lambda: nc.scalar.activation(out=lf, in_=lf, func=AF.Exp, scale=-1.0), r=['lf'], w=['lf'])
        t.op('act', lambda: nc.scalar.activation(out=lf, in_=lf, func=AF.Ln, bias=self.one_t[:], scale=1.0),
             r=['lf', 'c_one'], w=['lf'])
        t.op('dve', lambda: nc.vector.tensor_scalar(out=lf, in0=lf, scalar1=-1.0, scalar2=None, op0=ALU.mult),
             r=['lf'], w=['lf'])

    def pool_windows(self, g, ubuf, tmpa, tmpb, dT, w, first):
        nc, t = self.nc, self.t
        W = 2 ** (g + 1)
        n = 16 + w
        src, srck = ubuf, 'ubuf'
        bufs = [(tmpa, 'tmpa'), (tmpb, 'tmpb')]
        sh = 1
        k = 0
        while sh < W:
            dst, dstk = bufs[k % 2]
            t.op('dve', lambda src=src, dst=dst, sh=sh: nc.vector.tensor_tensor(
                out=dst[:, 2 * sh:n], in0=src[:, 2 * sh:n], in1=src[:, sh:n - sh], op=ALU.add),
                r=[srck], w=[dstk])
            src, srck = dst, dstk
            sh *= 2
            k += 1
        t.op('dve', lambda src=src: nc.vector.scalar_tensor_tensor(out=dT[:, 0:w], in0=src[:, 16:n], scalar=1.0 / W,
                                                                  in1=ubuf[:, 16:n], op0=ALU.mult, op1=ALU.subtract),
             r=[srck, 'ubuf'], w=['dT'])
        if first and W > 1:
            t.op('dve', lambda src=src: nc.vector.tensor_tensor(out=tmpa[:, 0:W - 1], in0=src[:, 16:16 + W - 1],
                                                               in1=self.invc[:, 0:W - 1], op=ALU.mult),
                 r=[srck, 'c_invc', 'tmpa', 'tmpb'], w=['tmpa', 'tmpb'])
            t.op('dve', lambda: nc.vector.tensor_tensor(out=dT[:, 0:W - 1], in0=tmpa[:, 0:W - 1], in1=ubuf[:, 16:16 + W - 1],
                                                        op=ALU.subtract), r=['tmpa', 'tmpb', 'ubuf'], w=['dT'])

    def even_prompt(self, i, hf, post=None):
        nc, t = self.nc, self.t
        NBP, PL, TW = self.NBP, self.PL, self.TW
        gb0 = hf * NBP
        pos0 = hf * PL
        with ExitStack() as es:
            sb = lambda n, sh, dt=F32: self.sb(es, n, sh, dt)
            QT = sb("QT", [128, 4, PL], BF16)
            poolT = sb("poolT", [128, 4, PL], BF16)
            NPR = max(1, NBP // 2)
            bias_all = sb("bias_all", [128, self.NBS, NPR, H])
            crefb = sb("crefb", [128, NBP, H])
            lf = sb("lf", [128, NBP, H])
            self.offt = sb("offt", [128, max(NBP, self.NBC + 1) + 1, H])
            bfb = sb("bfb", [128, H])
            pw = sb("pw", [128, 4, 128], BF16)
            pwst = sb("pwst", [128, 4, 128])
            pscale = sb("pscale", [128, 4])
            wo = None
            if hf == 0:
                t.op('pool', lambda: nc.gpsimd.memset(self.hist[:], 0.0), w=['hist'])
            with ExitStack() as e1:
                sb1 = lambda n, sh, dt=F32: self.sb(e1, n, sh, dt)
                stage = sb1("stage", [128, 8, 512])
                kst = [sb1("kst%d" % k, [128, 512]) for k in range(2)]
                vst = [sb1("vst%d" % k, [128, 512]) for k in range(2)]
                ubuf = sb1("ubuf", [128, 16 + TW])
                tmpa = sb1("tmpa", [128, 16 + TW])
                tmpb = sb1("tmpb", [128, 16 + TW])
                dT = sb1("dT", [128, TW], BF16)
                ust = sb1("ust", [128, 512])
                tiles = [(c0, TW) for c0 in range(0, PL, TW)]
                bank = [0]

                def nb():
                    bank[0] = (bank[0] + 1) % 4
                    return bank[0]

                def hk(c0, w):
                    return ['hT%d' % b for b in range(c0 // 128, (c0 + w) // 128)]

                for which, col0 in (("q", 0), ("k", 512)):
                    wt = self.load_w_bf16(e1, "w_" + which, self.even_w_in[:, col0:col0 + 512], 512, stage)
                    for (c0, w) in tiles:
                        for p in range(4):
                            bk = nb()
                            t.mms([lambda kc=kc, p=p, bk=bk, wt=wt, c0=c0, w=w: nc.tensor.matmul(
                                self.psb(bk)[:, 0:w], lhsT=wt[:, kc, p * 128:(p + 1) * 128], rhs=self.hT[:, kc, c0:c0 + w],
                                start=(kc == 0), stop=(kc == 7)) for kc in range(8)],
                                r=["w_" + which] + hk(c0, w), w=['ps%d' % bk])
                            dst = QT[:, p, c0:c0 + w] if which == "q" else self.KT[:, p, pos0 + c0:pos0 + c0 + w]
                            t.op('act', lambda dst=dst, bk=bk, w=w: nc.scalar.copy(out=dst, in_=self.psb(bk)[:, 0:w]),
                                 r=['ps%d' % bk], w=['QT' if which == "q" else 'KT'])
                self.chk("e_qk")
                t.dma('sp', out=bfb[:], in_=self.even_b_f.partition_broadcast(128), w=['bfb'], sem='misc')
                t.dma('sp', out=pwst[:], in_=self.pool_w.rearrange("g c d -> c g d"), w=['pwst'], sem='misc2')
                t.op('pool', lambda: nc.gpsimd.tensor_copy(out=pw[:], in_=pwst[:]), r=['pwst'], w=['pw'])
                for g in range(4):
                    t.dma('sp', out=pscale[:, g:g + 1], in_=self.pool_scale[g * 128:(g + 1) * 128].rearrange("(p o) -> p o", o=1),
                          w=['pscale'], sem='misc3_%d' % g)
                wkt = self.load_w_bf16(e1, "w_k2", self.even_w_in[:, 512:1024], 512, stage)
                wvt = self.load_w_bf16(e1, "w_v", self.even_w_in[:, 1024:1536], 512, stage)
                wft = self.load_w_bf16(e1, "w_f", self.even_w_in[:, 1536:1544], H, stage)
                wut = self.load_w_bf16(e1, "w_u", self.even_w_in[:, 1544:2056], 512, stage)
                for b in range(NBP):
                    t.mms([lambda kc=kc, b=b: nc.tensor.matmul(
                        self.psb(5)[:, b * H:(b + 1) * H], lhsT=self.hT[:, kc, b * 128:(b + 1) * 128], rhs=wft[:, kc, :],
                        start=(kc == 0), stop=(kc == 7)) for kc in range(8)], r=['w_f', 'hT%d' % b], w=['ps5'])
                self.logsig(self.psb(5)[:, 0:NBP * H], NBP, bfb, lf[:])
                t.dma('sp', out=self.nlf_p[i, pos0:pos0 + PL, :].rearrange("(b p) h -> p b h", p=128), in_=lf[:],
                      r=['lf'], sem='olf')
                self.chk("e_f")

                def kv_item(b, which):
                    wt, st, dstd = (wkt, kst, self.nk_p) if which == "k" else (wvt, vst, self.nv_p)
                    bk = nb()
                    t.mms([lambda kc=kc: nc.tensor.matmul(
                        self.psb(bk), lhsT=self.hT[:, kc, b * 128:(b + 1) * 128], rhs=wt[:, kc, :],
                        start=(kc == 0), stop=(kc == 7)) for kc in range(8)],
                        r=["w_k2" if which == "k" else "w_v", 'hT%d' % b], w=['ps%d' % bk])
                    k2 = b % 2
                    t.op('act', lambda: nc.scalar.copy(out=st[k2][:], in_=self.psb(bk)),
                         r=['ps%d' % bk], w=['%sst%d' % (which, k2)])
                    if which == "v":
                        t.op('dve', lambda: nc.vector.tensor_copy(
                            out=self.Vaug[:, gb0 + b, :, 0:64], in_=vst[k2][:].rearrange("p (h d) -> p h d", h=H)),
                            r=['vst%d' % k2], w=['Vaug'])
                    t.dma('sp', out=dstd[i, pos0 + b * 128:pos0 + (b + 1) * 128, :], in_=st[k2][:],
                          r=['%sst%d' % (which, k2)], sem='o%s%d' % (which, k2))

                def pool_item(ti, c0, w, g):
                    bk = nb()
                    t.mms([lambda kc=kc: nc.tensor.matmul(
                        self.psb(bk)[:, 0:w], lhsT=wut[:, kc, g * 128:(g + 1) * 128], rhs=self.hT[:, kc, c0:c0 + w],
                        start=(kc == 0), stop=(kc == 7)) for kc in range(8)], r=['w_u'] + hk(c0, w), w=['ps%d' % bk])
                    t.op('pool', lambda: nc.gpsimd.tensor_copy(out=ubuf[:, 0:16], in_=self.hist[:, g, :]),
                         r=['hist', 'tmpa', 'tmpb', 'dT'], w=['ubuf'])
                    t.op('act', lambda: nc.scalar.copy(out=ubuf[:, 16:16 + w], in_=self.psb(bk)[:, 0:w]),
                         r=['ps%d' % bk], w=['ubuf'])
                    t.op('pool', lambda: nc.gpsimd.tensor_copy(out=self.hist[:, g, 1:16], in_=ubuf[:, w + 1:w + 16]),
                         r=['ubuf'], w=['hist'])
                    self.pool_windows(g, ubuf, tmpa, tmpb, dT, w, first=(hf == 0 and ti == 0))
                    bk2 = nb()
                    t.mms([lambda: nc.tensor.matmul(self.psb(bk2)[:, 0:w], lhsT=pw[:, g, :], rhs=dT[:, 0:w],
                                                    start=True, stop=True)], r=['pw', 'dT'], w=['ps%d' % bk2])
                    t.op('act', lambda: nc.scalar.activation(
                        out=poolT[:, g, c0:c0 + w], in_=self.psb(bk2)[:, 0:w], func=AF.Copy, scale=pscale[:, g:g + 1]),
                        r=['ps%d' % bk2, 'pscale'], w=['poolT'])

                kv_list = [(b, which) for b in range(NBP) for which in ("k", "v")]
                pool_list = [(ti, c0, w, g) for ti, (c0, w) in enumerate(tiles) for g in range(4)]
                ratio = -(-len(kv_list) // len(pool_list))
                ki = 0
                for pi, it in enumerate(pool_list):
                    if pi == min(2, len(pool_list) - 1):
                        self.cumsum_blocks(lf[:], NBP, self.ctm[:, gb0:gb0 + NBP, :], None if hf == 0 else self.carry[:])
                    pool_item(*it)
                    for _ in range(ratio):
                        if ki < len(kv_list):
                            kv_item(*kv_list[ki])
                            ki += 1
                while ki < len(kv_list):
                    kv_item(*kv_list[ki])
                    ki += 1
                self.chk("e_pool")
                if hf == self.NH - 1:
                    b = NBP - 1
                    bk = nb()
                    t.mms([lambda kc=kc, bk=bk: nc.tensor.matmul(
                        self.psb(bk), lhsT=self.hT[:, kc, b * 128:(b + 1) * 128], rhs=wut[:, kc, :],
                        start=(kc == 0), stop=(kc == 7)) for kc in range(8)], r=['w_u', 'hT%d' % b], w=['ps%d' % bk])
                    t.op('act', lambda bk=bk: nc.scalar.copy(out=ust[:], in_=self.psb(bk)), r=['ps%d' % bk], w=['ust'])
                    t.dma('sp', out=self.npool_p[i], in_=ust[113:128, :], r=['ust'], sem='opool')
                t.barrier()
            with ExitStack() as e2:
                sb2 = lambda n, sh, dt=F32: self.sb(e2, n, sh, dt)
                stage = sb2("stage2", [128, 8, D])
                wo = self.load_w_bf16(e2, "w_o", self.even_w_out, D, stage)
                Pt = [sb2("Pt%d" % k, [128, 512], BF16) for k in range(3)]
                att = sb2("att", [128, 4, 512], BF16)
                attT = sb2("attT", [128, 4, 512], BF16)
                rec = sb2("rec", [128, 4])
                nkb = gb0 + NBP
                assert NBP % 2 == 0
                t.mms([lambda: nc.tensor.matmul(self.psb(5)[:, 0:NBP * H], lhsT=self.sel0[:],
                                                rhs=self.ctm[:, gb0:gb0 + NBP, :].rearrange("p b h -> p (b h)"),
                                                start=True, stop=True)], r=['ctm', 'sel0'], w=['ps5'])
                t.op('dve', lambda: nc.vector.tensor_copy(out=crefb[:].rearrange("p b h -> p (b h)"), in_=self.psb(5)[:, 0:NBP * H]),
                     r=['ps5'], w=['crefb'])
                cpair = crefb[:].rearrange("p (q two) h -> p q two h", two=2)[:, :, 1, :]
                for j in range(nkb):
                    t.op('dve', lambda j=j: nc.vector.tensor_tensor(
                        out=bias_all[:, j, :, :], in0=cpair, in1=bc(self.ctm[:, j:j + 1, :], [128, NPR, H]),
                        op=ALU.subtract), r=['crefb', 'ctm'], w=['bias'])
                self.chk("e_bias")
                pend_e = []
                for (c0, w) in [(c0, TW) for c0 in range(0, PL, TW)]:
                    s0 = c0 // 128
                    nsb = w // 128
                    units = [(h, j) for h in range(H) for j in range(gb0 + s0 + nsb)]
                    LA = 2
                    jl = gb0 + s0 + nsb - 1

                    def qk_exp(u):
                        h, j = units[u]
                        p, bp = h // 2, 64 * (h % 2)
                        sbk = u % 3
                        smin = max(0, j - (gb0 + s0))
                        t.mms([lambda: nc.tensor.matmul(
                            self.psb(sbk)[:, smin * 128:nsb * 128], lhsT=self.KT[bp:bp + 64, p, j * 128:(j + 1) * 128],
                            rhs=QT[bp:bp + 64, p, c0 + smin * 128:c0 + nsb * 128], start=True, stop=True)],
                            r=['KT', 'QT'], w=['ps%d' % sbk])
                        for pp in range(smin // 2, nsb // 2):
                            lo, hi = max(smin, 2 * pp), 2 * pp + 2
                            t.op('act', lambda lo=lo, hi=hi, pp=pp: nc.scalar.activation(
                                out=Pt[sbk][:, lo * 128:hi * 128], in_=self.psb(sbk)[:, lo * 128:hi * 128],
                                func=AF.Exp, bias=bias_all[:, j, s0 // 2 + pp, h:h + 1], scale=0.125),
                                r=['ps%d' % sbk, 'bias'], w=['Pt%d_%d' % (sbk, si) for si in range(lo, hi)])
                        sd = j - (gb0 + s0)
                        if sd >= 0:
                            t.op('pool', lambda: nc.gpsimd.tensor_tensor(
                                out=Pt[sbk][:, sd * 128:(sd + 1) * 128], in0=Pt[sbk][:, sd * 128:(sd + 1) * 128],
                                in1=self.trimask[:], op=ALU.mult), r=['Pt%d_%d' % (sbk, sd), 'trimask'], w=['Pt%d_%d' % (sbk, sd)])

                    def pv(u):
                        h, j = units[u]
                        sbk = u % 3
                        ob = 3 + (h % 2)
                        smin = max(0, j - (gb0 + s0))
                        fns = []
                        for si in range(smin, nsb):
                            fns.append(lambda si=si, st=(j == 0 and si == smin): nc.tensor.matmul(
                                self.psb(ob)[:, si * 65:(si + 1) * 65], lhsT=Pt[sbk][:, si * 128:(si + 1) * 128],
                                rhs=self.Vaug[:, j, h, :], start=st, stop=(j == gb0 + s0 + si), skip_group_check=True))
                        t.mms(fns, r=['Pt%d_%d' % (sbk, si) for si in range(smin, nsb)] + ['Vaug', 'vones'], w=['ps%d' % ob])
                        if j == jl:
                            Ov = self.psb(ob)[:, 0:nsb * 65].rearrange("p (s e) -> p s e", e=65)
                            t.op('dve', lambda: nc.vector.reciprocal(out=rec[:, 0:nsb].unsqueeze(2), in_=Ov[:, :, 64:65]),
                                 r=['ps%d' % ob], w=['rec'])
                            t.op('dve', lambda: nc.vector.tensor_tensor(
                                out=att[:, 0:nsb, h * 64:(h + 1) * 64], in0=Ov[:, :, 0:64],
                                in1=bc(rec[:, 0:nsb].unsqueeze(2), [128, nsb, 64]), op=ALU.mult),
                                r=['ps%d' % ob, 'rec'], w=['att%d' % h])

                    for u in range(len(units) + LA):
                        if u < len(units):
                            qk_exp(u)
                        if u - LA >= 0:
                            pv(u - LA)
                        if u == len(units) // 2 and post is not None and pend_e:
                            post(pend_e)
                            pend_e = []
                    self.chk("e_att")
                    for si in range(nsb):
                        bank = 6
                        self.tq += 1
                        pst = self.psb16(bank)
                        t.mms([lambda c=c, si=si, pst=pst: nc.tensor.transpose(
                            out=pst[:, c * 128:(c + 1) * 128], in_=att[:, si, c * 128:(c + 1) * 128], identity=self.identb[:])
                            for c in range(4)], r=['att%d' % h for h in range(H)] + ['identb'], w=['ps%d' % bank])
                        t.op('act', lambda si=si, pst=pst: nc.scalar.copy(
                            out=attT[:, :, si * 128:(si + 1) * 128], in_=pst[:, 0:512].rearrange("p (c t) -> p c t", c=4)),
                            r=['ps%d' % bank], w=['attT%d' % si])
                        b = s0 + si
                        for dh in range(2):
                            yb = 5 if dh == 0 else 7
                            fns = []
                            for c in range(8):
                                lh = attT[:, c, si * 128:(si + 1) * 128] if c < 4 else poolT[:, c - 4, c0 + si * 128:c0 + (si + 1) * 128]
                                fns.append(lambda c=c, lh=lh, dh=dh, yb=yb: nc.tensor.matmul(
                                    self.psb(yb), lhsT=lh, rhs=wo[:, c, dh * 512:(dh + 1) * 512], start=(c == 0), stop=(c == 7)))
                            t.mms(fns, r=['attT%d' % si, 'poolT', 'w_o'], w=['ps%d' % yb])
                            xs = self.x[:, b, dh * 512:(dh + 1) * 512]
                            t.op('dve', lambda xs=xs, yb=yb: nc.vector.tensor_tensor(out=xs, in0=self.psb(yb), in1=xs, op=ALU.add),
                                 r=['ps%d' % yb, 'x%d_%d' % (b, dh)], w=['x%d_%d' % (b, dh)])
                    if post is not None:
                        if pend_e:
                            post(pend_e)
                        pend_e = list(range(s0, s0 + nsb))
                if post is not None and pend_e:
                    post(pend_e)
                t.barrier()

    def even_sample(self, post=None):
        nc, t = self.nc, self.t
        PL, NBC, NS, T = self.PL, self.NBC, self.NS, self.T
        SBk = self.SB
        hc0 = PL
        with ExitStack() as es:
            sb = lambda n, sh, dt=F32: self.sb(es, n, sh, dt)
            wq = self.load_w_bf16(es, "w_q", self.even_w_in[:, 0:512], 512, None)
            wk = self.load_w_bf16(es, "w_k", self.even_w_in[:, 512:1024], 512, None)
            wv = self.load_w_bf16(es, "w_v", self.even_w_in[:, 1024:1536], 512, None)
            wf = self.load_w_bf16(es, "w_f", self.even_w_in[:, 1536:1544], H, None)
            wu = self.load_w_bf16(es, "w_u", self.even_w_in[:, 1544:2056], 512, None)
            wo = self.load_w_bf16(es, "w_o", self.even_w_out, D, None)
            QTs = sb("QTs", [128, 4, 128], BF16)
            KTs = sb("KTs", [128, 4, 128], BF16)
            KTc = sb("KTc", [128, 4, self.P], BF16)
            Vc = sb("Vc", [128, NBC + 1, H, 65], BF16)
            cbf = sb("cbf", [128, NBC, 512], BF16)
            lfa = sb("lfa", [128, NBC + 1, H])
            call = sb("call", [128, NBC + 1, H])
            self.offt = sb("offt_s", [128, NBC + 3, H])
            bias = sb("bias_s", [128, NBC + 1, H])
            crefb = sb("crefb_s", [128, H])
            bfb = sb("bfb_s", [128, H])
            Pt = [sb("Pts%d" % k, [128, 32], BF16) for k in range(3)]
            atts = sb("atts", [32, 512], BF16)
            mixT = sb("mixTs", [128, 8, 128], BF16)
            rec = sb("recs", [32, 1])
            kst = sb("ksts", [32, 512])
            vst = sb("vsts", [32, 512])
            ust = sb("usts", [32, 512])
            hst = sb("hsts", [16, 512])
            uTs = sb("uTs", [128, 4, 128])
            dTs = sb("dTs", [128, 4, 128], BF16)
            ubuf = sb("ubufs", [128, 48])
            tmpa = sb("tmpas", [128, 48])
            tmpb = sb("tmpbs", [128, 48])
            dtmp = sb("dtmps", [128, 32], BF16)
            pw = sb("pws", [128, 4, 128], BF16)
            pwst = sb("pwsts", [128, 4, 128])
            pscale = sb("pscales", [128, 4])
            t.dma('sp', out=bfb[:], in_=self.even_b_f.partition_broadcast(128), w=['bfb'], sem='misc')
            t.dma('sp', out=pwst[:], in_=self.pool_w.rearrange("g c d -> c g d"), w=['pwst'], sem='misc2')
            t.op('pool', lambda: nc.gpsimd.tensor_copy(out=pw[:], in_=pwst[:]), r=['pwst'], w=['pw'])
            for g in range(4):
                t.dma('sp', out=pscale[:, g:g + 1], in_=self.pool_scale[g * 128:(g + 1) * 128].rearrange("(p o) -> p o", o=1),
                      w=['pscale'], sem='misc3_%d' % g)
            t.op('pool', lambda: nc.gpsimd.memset(Vc[:, :, :, 64:65], 1.0), w=['Vc1'])
            hk = ['hT%d' % SBk]
            for which, wt, dst in (("q", wq, QTs), ("k", wk, KTs), ("u", wu, uTs)):
                for p in range(4):
                    bk = p
                    t.mms([lambda kc=kc, p=p, bk=bk, wt=wt: nc.tensor.matmul(
                        self.psb(bk)[:, 0:128], lhsT=wt[:, kc, p * 128:(p + 1) * 128], rhs=self.hT[:, kc, hc0:hc0 + 128],
                        start=(kc == 0), stop=(kc == 7)) for kc in range(8)], r=['w_q', 'w_k', 'w_u'] + hk,
                        w=['ps%d' % bk])
                    t.op('act', lambda dst=dst, p=p, bk=bk: nc.scalar.copy(out=dst[:, p, :], in_=self.psb(bk)[:, 0:128]),
                         r=['ps%d' % bk], w=['s_' + which])
            oq = 0
            sq = 0
            for si in range(NS):
                cs = hc0 + si * T
                for which, wt, st, dstd in (("k", wk, kst, self.nk_s), ("v", wv, vst, self.nv_s), ("u", wu, ust, None)):
                    bk = 4
                    t.mms([lambda kc=kc, wt=wt: nc.tensor.matmul(
                        self.psb(4)[0:32, :], lhsT=self.hT[:, kc, cs:cs + T], rhs=wt[:, kc, :],
                        start=(kc == 0), stop=(kc == 7)) for kc in range(8)], r=['w_k', 'w_v', 'w_u'] + hk, w=['ps4'])
                    t.op('act', lambda st=st: nc.scalar.copy(out=st[:], in_=self.psb(4)[0:32, :]), r=['ps4'], w=['st_' + which])
                    if which == "v":
                        t.op('pool', lambda: nc.gpsimd.tensor_copy(out=Vc[0:32, NBC, :, 0:64],
                                                                   in_=vst[:].rearrange("p (h d) -> p h d", h=H)),
                             r=['st_v'], w=['Vcn'])
                    if dstd is not None:
                        t.dma('sp', out=dstd[si], in_=st[:], r=['st_' + which], sem='os_' + which)
                    else:
                        t.dma('sp', out=self.npool_s[si], in_=st[17:32, :], r=['st_' + which], sem='os_' + which)
                t.dma('pool', out=cbf[:], in_=self.ck[si].rearrange("(b p) c -> p b c", p=128), w=['cbf'], sem='cin')
                for b in range(NBC):
                    bank = 6
                    self.tq += 1
                    pst = self.psb16(bank)
                    t.mms([lambda c=c, b=b, pst=pst: nc.tensor.transpose(out=pst[:, c * 128:(c + 1) * 128],
                                                                      in_=cbf[:, b, c * 128:(c + 1) * 128], identity=self.identb[:])
                           for c in range(4)], r=['cbf', 'identb'], w=['ps%d' % bank])
                    t.op('act', lambda b=b, pst=pst: nc.scalar.copy(out=KTc[:, :, b * 128:(b + 1) * 128],
                                                                  in_=pst[:, 0:512].rearrange("p (c t) -> p c t", c=4)),
                         r=['ps%d' % bank], w=['KTc'])
                for b in range(NBC):
                    t.dma('pool', out=Vc[:, b, :, 0:64], in_=self.cv[si][b * 128:(b + 1) * 128, :].rearrange("p (h d) -> p h d", h=H),
                          w=['Vcc'], sem='cinv')
                t.op('dve', lambda: nc.vector.memset(lfa[:], 0.0), w=['lf'])
                t.dma('sp', out=lfa[:, 0:NBC, :], in_=self.clf[si].rearrange("(b p) h -> p b h", p=128), w=['lf'], sem='cin2')
                t.mms([lambda kc=kc: nc.tensor.matmul(self.psb(5)[0:32, 0:H], lhsT=self.hT[:, kc, cs:cs + T], rhs=wf[:, kc, :],
                                                      start=(kc == 0), stop=(kc == 7)) for kc in range(8)], r=['w_f'] + hk, w=['ps5'])
                nc_ = nc
                t.op('dve', lambda: nc_.vector.tensor_tensor(out=lfa[0:32, NBC, :], in0=self.psb(5)[0:32, 0:H], in1=bfb[0:32, :], op=ALU.add),
                     r=['ps5', 'bfb', 'lf'], w=['lf'])
                t.op('act', lambda: nc_.scalar.activation(out=lfa[0:32, NBC, :], in_=lfa[0:32, NBC, :], func=AF.Exp, scale=-1.0),
                     r=['lf'], w=['lf'])
                t.op('act', lambda: nc_.scalar.activation(out=lfa[0:32, NBC, :], in_=lfa[0:32, NBC, :], func=AF.Ln,
                                                         bias=self.one_t[0:32, :], scale=1.0), r=['lf', 'c_one'], w=['lf'])
                t.op('dve', lambda: nc_.vector.tensor_scalar(out=lfa[0:32, NBC, :], in0=lfa[0:32, NBC, :], scalar1=-1.0, scalar2=None,
                                                            op0=ALU.mult), r=['lf'], w=['lf'])
                t.dma('sp', out=self.nlf_s[si], in_=lfa[0:32, NBC, :], r=['lf'], sem='os_lf')
                self.cumsum_blocks(lfa[:], NBC + 1, call[:], None, update_carry=False)
                t.mms([lambda: nc.tensor.matmul(self.psb(5)[:, 0:H], lhsT=self.sel16[:], rhs=call[:, NBC, :], start=True, stop=True)],
                      r=['ctm', 'sel16'], w=['ps5'])
                t.op('dve', lambda: nc.vector.tensor_copy(out=crefb[:], in_=self.psb(5)[:, 0:H]), r=['ps5'], w=['crefb'])
                t.op('dve', lambda: nc.vector.tensor_tensor(out=bias[:], in0=bc(crefb[:].unsqueeze(1), [128, NBC + 1, H]), in1=call[:],
                                                            op=ALU.subtract), r=['crefb', 'ctm'], w=['bias'])
                units = [(h, j) for h in range(H) for j in range(NBC + 1)]
                LA = 2

                def qk_exp(u):
                    h, j = units[u]
                    p, bp = h // 2, 64 * (h % 2)
                    sbk = u % 3
                    nk = 128 if j < NBC else T
                    lh = KTc[bp:bp + 64, p, j * 128:(j + 1) * 128] if j < NBC else KTs[bp:bp + 64, p, si * T:(si + 1) * T]
                    t.mms([lambda: nc.tensor.matmul(
                        self.psb(sbk)[0:nk, 0:T], lhsT=lh, rhs=QTs[bp:bp + 64, p, si * T:(si + 1) * T], start=True, stop=True)],
                        r=['KTc', 's_k', 's_q'], w=['ps%d' % sbk])
                    t.op('act', lambda: nc.scalar.activation(
                        out=Pt[sbk][0:nk, :], in_=self.psb(sbk)[0:nk, 0:T], func=AF.Exp, bias=bias[0:nk, j, h:h + 1], scale=0.125),
                        r=['ps%d' % sbk, 'bias'], w=['Pts%d' % sbk])
                    if j == NBC:
                        t.op('pool', lambda: nc.gpsimd.tensor_tensor(out=Pt[sbk][0:T, :], in0=Pt[sbk][0:T, :],
                                                                     in1=self.trimask[0:T, 0:T], op=ALU.mult),
                             r=['Pts%d' % sbk, 'trimask'], w=['Pts%d' % sbk])

                def pv(u):
                    h, j = units[u]
                    sbk = u % 3
                    ob = 3 + (h % 2)
                    nk = 128 if j < NBC else T
                    t.mms([lambda: nc.tensor.matmul(
                        self.psb(ob)[0:32, 0:65], lhsT=Pt[sbk][0:nk, :], rhs=Vc[0:nk, j, h, :], start=(j == 0), stop=(j == NBC))],
                        r=['Pts%d' % sbk, 'Vcc', 'Vcn', 'Vc1'], w=['ps%d' % ob])
                    if j == NBC:
                        t.op('dve', lambda: nc.vector.reciprocal(out=rec[:], in_=self.psb(ob)[0:32, 64:65]), r=['ps%d' % ob], w=['recs'])
                        t.op('dve', lambda: nc.vector.tensor_scalar(out=atts[:, h * 64:(h + 1) * 64], in0=self.psb(ob)[0:32, 0:64],
                                                                    scalar1=rec[:, 0:1], scalar2=None, op0=ALU.mult),
                             r=['ps%d' % ob, 'recs'], w=['atts'])

                for u in range(len(units) + LA):
                    if u < len(units):
                        qk_exp(u)
                    if u - LA >= 0:
                        pv(u - LA)
                bank = 6
                self.tq += 1
                pst = self.psb16(bank)
                t.mms([lambda c=c, pst=pst: nc.tensor.transpose(out=pst[:, c * 32:(c + 1) * 32], in_=atts[:, c * 128:(c + 1) * 128],
                                                                identity=self.identb[0:32, 0:32]) for c in range(4)],
                      r=['atts', 'identb'], w=['ps%d' % bank])
                t.op('act', lambda pst=pst: nc.scalar.copy(out=mixT[:, 0:4, si * T:(si + 1) * T],
                                                          in_=pst[:, 0:128].rearrange("p (c t) -> p c t", c=4)),
                     r=['ps%d' % bank], w=['mixTs'])
                t.dma('sp', out=hst[0:15, :], in_=self.spool[si], w=['hst'], sem='cin3')
                for g in range(4):
                    t.mms([lambda g=g: nc.tensor.transpose(out=self.psb(4)[:, 0:15], in_=hst[0:15, g * 128:(g + 1) * 128],
                                                           identity=self.identf[0:15, 0:15])], r=['hst', 'identf'], w=['ps4'])
                    t.op('act', lambda: nc.scalar.copy(out=ubuf[:, 1:16], in_=self.psb(4)[:, 0:15]),
                         r=['ps4', 'tmpa', 'tmpb', 'dT'], w=['ubuf'])
                    t.op('pool', lambda g=g: nc.gpsimd.tensor_copy(out=ubuf[:, 16:16 + T], in_=uTs[:, g, si * T:(si + 1) * T]),
                         r=['s_u'], w=['ubuf'])
                    self.pool_windows(g, ubuf, tmpa, tmpb, dtmp, T, first=False)
                    t.op('pool', lambda g=g: nc.gpsimd.tensor_copy(out=dTs[:, g, si * T:(si + 1) * T], in_=dtmp[:]),
                         r=['dT'], w=['dTs'])
            for g in range(4):
                t.mms([lambda g=g: nc.tensor.matmul(self.psb(g)[:, 0:128], lhsT=pw[:, g, :], rhs=dTs[:, g, :], start=True, stop=True)],
                      r=['pw', 'dTs'], w=['ps%d' % g])
                t.op('act', lambda g=g: nc.scalar.activation(out=mixT[:, 4 + g, :], in_=self.psb(g)[:, 0:128], func=AF.Copy,
                                                             scale=pscale[:, g:g + 1]), r=['ps%d' % g, 'pscale'], w=['mixTs'])
            for dh in range(2):
                t.mms([lambda c=c, dh=dh: nc.tensor.matmul(self.psb(5), lhsT=mixT[:, c, :], rhs=wo[:, c, dh * 512:(dh + 1) * 512],
                                                          start=(c == 0), stop=(c == 7)) for c in range(8)],
                      r=['mixTs', 'w_o'], w=['ps5'])
                xs = self.x[:, SBk, dh * 512:(dh + 1) * 512]
                t.op('dve', lambda xs=xs: nc.vector.tensor_tensor(out=xs, in0=self.psb(5), in1=xs, op=ALU.add),
                     r=['ps5', 'x%d_%d' % (SBk, dh)], w=['x%d_%d' % (SBk, dh)])
            if post is not None:
                post([SBk])
            t.barrier()

    def odd(self, i, hf, blocks, post=None, gk=None):
        nc, t = self.nc, self.t
        T, NS = self.T, self.NS
        with ExitStack() as es:
            sb = lambda n, sh, dt=F32: self.sb(es, n, sh, dt)
            w_in = [sb("wsg_in%d" % q, [128, 8, 512], BF16) for q in range(4)]
            wo = [sb("wsg_o%d" % dh, [128, 8, 512], BF16) for dh in range(2)]
            wsn = sb("wsn", [128, 8, 128])
            wsT = sb("wsT", [128, 8, 128], BF16)
            wsTs = sb("wsTs", [128, 8, 128], BF16)
            bsn = sb("bsn", [8, 128])
            bsT = sb("bsT", [128, 8])
            bsTs = sb("bsTs", [128, 8])
            gsg = sb("gsg", [128, D])
            t.dma('sp', out=gsg[:], in_=self.sgu_norm_g.partition_broadcast(128), w=['gsg'], sem='misc')
            t.dma('sp', out=wsn[:], in_=self.sgu_w_s.rearrange("g t s -> t g s"), w=['wsn'], sem='misc2')
            t.dma('sp', out=bsn[:], in_=self.sgu_b_s, w=['bsn'], sem='misc4')
            with ExitStack() as e0:
                first = not (hasattr(self, 'scr') and 'wsg_in0' in self.scr)
                stage = self.sb(e0, "stage_o", [128, 8, 512]) if first else None
                for q in range(4):
                    self.load_w_bf16(es, "wsg_in%d" % q, self.sgu_w_in[:, q * 512:(q + 1) * 512], 512, stage, wt=w_in[q])
                for dh in range(2):
                    self.load_w_bf16(es, "wsg_o%d" % dh, self.sgu_w_out[:, dh * 512:(dh + 1) * 512], 512, stage, wt=wo[dh])
                if first:
                    t.barrier()
            zu = [sb("zu%d" % k, [128, D]) for k in range(2)]
            zv = [sb("zv%d" % k, [128, D]) for k in range(2)]
            zvn = [sb("zvn%d" % k, [128, D], BF16) for k in range(2)]
            zvf = sb("zvf", [128, D])
            gated = [sb("gated%d" % k, [128, D], BF16) for k in range(2)]
            gT = [sb("gT%d" % k, [128, 8, 128], BF16) for k in range(2)]
            rs = sb("rs_o", [128, 4])
            rso = sb("rso_o", [128, 4])
            def prep_ws():
                t.op('pool', lambda: nc.gpsimd.memset(wsn[0:64, :, 64:128], 0.0), r=['wsn'], w=['wsn'])
                for g in range(8):
                    t.mms([lambda g=g: nc.tensor.transpose(out=self.psb(g % 4)[:, 0:128], in_=wsn[:, g, :], identity=self.identf[:])],
                          r=['wsn', 'identf'], w=['ps%d' % (g % 4)])
                    t.op('act', lambda g=g: nc.scalar.copy(out=wsT[:, g, :], in_=self.psb(g % 4)[:, 0:128]), r=['ps%d' % (g % 4)], w=['wsT'])
                t.mms([lambda: nc.tensor.transpose(out=self.psb(4)[:, 0:8], in_=bsn[:], identity=self.identf[0:8, 0:8])],
                      r=['bsn', 'identf'], w=['ps4'])
                t.op('act', lambda: nc.scalar.copy(out=bsT[:], in_=self.psb(4)[:, 0:8]), r=['ps4'], w=['bsT'])
                if self.SB in blocks:
                    t.op('pool', lambda: nc.gpsimd.memset(wsTs[:], 0.0), w=['wsTs'])
                    for si in range(NS):
                        t.dma('sp', out=wsTs[si * T:(si + 1) * T, :, si * T:(si + 1) * T], in_=wsT[0:T, :, 0:T], r=['wsT'], w=['wsTs'],
                              sem='sm%d' % si)
                        t.dma('sp', out=bsTs[si * T:(si + 1) * T, :], in_=bsT[0:T, :], r=['bsT'], w=['bsTs'], sem='sn%d' % si)
            t.op('dve', lambda: nc.vector.memset(rs[:], 0.0), w=['rs_o'])

            def stageA(bi, cts=(0, 1, 2, 3)):
                b = blocks[bi]
                k = bi % 2
                hcol = b * 128
                for ct in cts:
                    bk = ct
                    t.mms([lambda kc=kc, ct=ct, bk=bk: nc.tensor.matmul(
                        self.psb(bk), lhsT=self.hT[:, kc, hcol:hcol + 128], rhs=w_in[ct][:, kc, :],
                        start=(kc == 0), stop=(kc == 7)) for kc in range(8)], r=['wsg_in%d' % ct, 'hT%d' % b], w=['ps%d' % bk])
                    dst = (zu[k] if ct < 2 else zv[k])[:, (ct % 2) * 512:(ct % 2 + 1) * 512]
                    t.op('act', lambda bk=bk, dst=dst: nc.scalar.activation(out=dst, in_=self.psb(bk), func=AF.Gelu_apprx_tanh),
                         r=['ps%d' % bk], w=['z%d_%d' % (k, ct)])

            def stageB(bi):
                b = blocks[bi]
                k = bi % 2
                samp = (b == self.SB)
                t.op('act', lambda: nc.scalar.activation(out=self.junk[:], in_=zv[k][:], func=AF.Square, accum_out=rs[:, k:k + 1]),
                     r=['z%d_2' % k, 'z%d_3' % k, 'rs_o'], w=['rs_w%d' % k])
                t.op('dve', lambda: nc.vector.tensor_scalar(out=rso[:, k:k + 1], in0=rs[:, k:k + 1], scalar1=1.0 / D, scalar2=EPS,
                                                            op0=ALU.mult, op1=ALU.add), r=['rs_w%d' % k], w=['rso%d' % k])
                t.op('pool', lambda: nc.gpsimd.tensor_tensor(out=rso[:, 2 + k:3 + k], in0=rso[:, k:k + 1], in1=self.mhalf[:], op=ALU.pow),
                     r=['rso%d' % k, 'c_mhalf'], w=['rsr%d' % k])
                t.op('dve', lambda: nc.vector.memset(rs[:, k:k + 1], 0.0), r=['rs_w%d' % k, 'rso%d' % k], w=['rs_o'])
                if samp:
                    t.op('dve', lambda: nc.vector.scalar_tensor_tensor(out=zvf[:], in0=zv[k][:], scalar=rso[:, 2 + k:3 + k], in1=gsg[:],
                                                                       op0=ALU.mult, op1=ALU.mult),
                         r=['z%d_2' % k, 'z%d_3' % k, 'rsr%d' % k, 'gsg'], w=['zvf'])
                    t.op('pool', lambda: nc.gpsimd.tensor_copy(out=zvn[k][:], in_=zvf[:]), r=['zvf'], w=['zvn%d' % k])
                    t.dma('sp', out=self.nz_s, in_=zvf[:], r=['zvf'], sem='oz')
                else:
                    t.op('dve', lambda: nc.vector.scalar_tensor_tensor(out=zvn[k][:], in0=zv[k][:], scalar=rso[:, 2 + k:3 + k], in1=gsg[:],
                                                                       op0=ALU.mult, op1=ALU.mult),
                         r=['z%d_2' % k, 'z%d_3' % k, 'rsr%d' % k, 'gsg'], w=['zvn%d' % k])

            def stageB2(bi):
                b = blocks[bi]
                k = bi % 2
                samp = (b == self.SB)
                wst = wsTs if samp else wsT
                bst = bsTs if samp else bsT
                for g in range(8):
                    mb = 4 + g // 4
                    t.mms([lambda g=g, mb=mb: nc.tensor.matmul(self.psb(mb)[:, (g % 4) * 128:(g % 4 + 1) * 128], lhsT=wst[:, g, :],
                                                            rhs=zvn[k][:, g * 128:(g + 1) * 128], start=True, stop=True)],
                          r=['wsT', 'wsTs', 'zvn%d' % k], w=['ps%d' % mb])
                    if g % 4 == 3:
                        for gg in range(g - 3, g + 1):
                            t.op('dve', lambda gg=gg, mb=mb: nc.vector.scalar_tensor_tensor(
                                out=gated[k][:, gg * 128:(gg + 1) * 128], in0=self.psb(mb)[:, (gg % 4) * 128:(gg % 4 + 1) * 128],
                                scalar=bst[:, gg:gg + 1], in1=zu[k][:, gg * 128:(gg + 1) * 128], op0=ALU.add, op1=ALU.mult),
                                r=['ps%d' % mb, 'bsT', 'bsTs', 'z%d_0' % k, 'z%d_1' % k], w=['gated%d_%d' % (k, gg)])

            def stageB3(bi):
                b = blocks[bi]
                k = bi % 2
                bank = 6
                pst = self.psb16(bank)
                t.mms([lambda c=c: nc.tensor.transpose(out=pst[:, c * 128:(c + 1) * 128], in_=gated[k][:, c * 128:(c + 1) * 128],
                                                       identity=self.identb[:]) for c in range(8)],
                      r=['gated%d_%d' % (k, g) for g in range(8)] + ['identb'], w=['ps%d' % bank])
                t.op('act', lambda: nc.scalar.copy(out=gT[k][:], in_=pst.rearrange("p (c t) -> p c t", c=8)),
                     r=['ps%d' % bank], w=['gT%d' % k])
                for dh in range(2):
                    yb = 4 + dh
                    t.mms([lambda c=c, dh=dh, yb=yb: nc.tensor.matmul(self.psb(yb), lhsT=gT[k][:, c, :], rhs=wo[dh][:, c, :],
                                                                   start=(c == 0), stop=(c == 7)) for c in range(8)],
                          r=['gT%d' % k, 'wsg_o%d' % dh], w=['ps%d' % yb])
                    xs = self.x[:, b, dh * 512:(dh + 1) * 512]
                    t.op('dve', lambda xs=xs, yb=yb: nc.vector.tensor_tensor(out=xs, in0=self.psb(yb), in1=xs, op=ALU.add),
                         r=['ps%d' % yb, 'x%d_%d' % (b, dh)], w=['x%d_%d' % (b, dh)])

            nb_ = len(blocks)
            stageA(0)
            prep_ws()
            for bi in range(nb_):
                stageB(bi)
                ctx = self.norm_pre(blocks[bi - 2], gk) if (post is not None and bi >= 2) else None
                if bi + 1 < nb_:
                    stageA(bi + 1, (0, 1))
                stageB2(bi)
                if ctx is not None:
                    self.norm_post(ctx)
                if bi + 1 < nb_:
                    stageA(bi + 1, (2, 3))
                elif post is not None and nb_ >= 2:
                    post([blocks[nb_ - 2]])
                stageB3(bi)
            if post is not None:
                post([blocks[nb_ - 1]])
            t.barrier()

    def final(self, i, hf, blocks):
        nc, t = self.nc, self.t
        with ExitStack() as es:
            yst = [self.sb(es, "yst%d" % k, [128, D]) for k in range(2)]
            t.dma('sp', out=self.gb[:], in_=self.final_g.partition_broadcast(128), w=['gb'], sem='gb')
            self.rstd_of([self.x[:, b, :] for b in blocks], [self.xk(b) for b in blocks])
            for j, b in enumerate(blocks):
                k = j % 2
                t.op('dve', lambda b=b, j=j, k=k: nc.vector.scalar_tensor_tensor(
                    out=yst[k][:], in0=self.x[:, b, :], scalar=self.rstd[:, j:j + 1], in1=self.gb[:],
                    op0=ALU.mult, op1=ALU.mult), r=self.xk(b) + ['rstd', 'gb'], w=['yst%d' % k])
                if b == self.SB:
                    dst = self.y_s
                else:
                    p0 = hf * self.PL + b * 128
                    dst = self.y_p[i, p0:p0 + 128, :]
                t.dma('sp', out=dst, in_=yst[k][:], r=['yst%d' % k], sem='oy%d' % k)
            t.barrier()


_CACHE = {}


def _get_prog(key):
    if key not in _CACHE:
        _CACHE[key] = Prog(*key)
    return _CACHE[key]


def run_cores(prog, per_core_inputs):
    res = run_bass_kernel_spmd(prog.nc, per_core_inputs, core_ids=list(range(len(per_core_inputs))))
    return res.results


def kernel(x_prompt, x_sample, cache_k, cache_v, cache_logf, state_pool, norm_g, ffn_w_in, ffn_w_down,
           even_w_in, even_b_f, pool_w, pool_scale, even_w_out, sgu_w_in, sgu_norm_g, sgu_w_s, sgu_b_s,
           sgu_w_out, final_g):
    NC = 8
    f = lambda a: np.ascontiguousarray(np.asarray(a, dtype=np.float32))
    x_prompt, x_sample = f(x_prompt), f(x_sample)
    B, S, _ = x_prompt.shape
    Bs, T, _ = x_sample.shape
    P = cache_k.shape[2]
    NP, NS = B // NC, Bs // NC
    prog = _get_prog((NP, S, 1024, NS, T, P))
    shared = {
        "norm_g": f(norm_g), "ffn_w_in": f(ffn_w_in), "ffn_w_down": f(ffn_w_down),
        "even_w_in": f(even_w_in)[0], "even_b_f": f(even_b_f)[0], "pool_w": f(pool_w)[0],
        "pool_scale": f(pool_scale)[0], "even_w_out": f(even_w_out)[0], "sgu_w_in": f(sgu_w_in)[0],
        "sgu_norm_g": f(sgu_norm_g)[0], "sgu_w_s": f(sgu_w_s)[0], "sgu_b_s": f(sgu_b_s)[0],
        "sgu_w_out": f(sgu_w_out)[0], "final_g": f(final_g),
    }
    ck, cv, clf, sp = f(cache_k)[0], f(cache_v)[0], f(cache_logf)[0], f(state_pool)[0]
    in_maps = []
    for c in range(NC):
        m = dict(shared)
        m["xp"] = x_prompt[c * NP:(c + 1) * NP]
        m["xs"] = x_sample[c * NS:(c + 1) * NS].reshape(NS * T, D)
        m["ck"] = ck[c * NS:(c + 1) * NS].reshape(NS, P, 512)
        m["cv"] = cv[c * NS:(c + 1) * NS].reshape(NS, P, 512)
        m["clf"] = clf[c * NS:(c + 1) * NS]
        m["spool"] = sp[c * NS:(c + 1) * NS]
        in_maps.append(m)
    res = run_cores(prog, in_maps)
    cat = lambda k: np.concatenate([r[k] for r in res], axis=0)
    y_p = cat("y_p")
    y_s = cat("y_s").reshape(Bs, T, D)
    nk_p = cat("nk_p").reshape(1, B, S, H, DH)
    nv_p = cat("nv_p").reshape(1, B, S, H, DH)
    nlf_p = cat("nlf_p").reshape(1, B, S, H)
    npool_p = cat("npool_p").reshape(1, B, 15, 512)
    nk_s = cat("nk_s").reshape(1, Bs, T, H, DH)
    nv_s = cat("nv_s").reshape(1, Bs, T, H, DH)
    nlf_s = cat("nlf_s").reshape(1, Bs, T, H)
    npool_s = cat("npool_s").reshape(1, Bs, 15, 512)
    nz_s = cat("nz_s").reshape(1, Bs, T, D)
    return (y_p, y_s, nk_p, nv_p, nlf_p, npool_p, nk_s, nv_s, nlf_s, npool_s, nz_s)
```
